# Optimizing a Trainium2 kernel written in Bass

```python
import jax, jax.numpy as jnp
from jax import lax
import numpy as np

D_MODEL = 4096
BATCH = 4
SEQ = 2048
DEPTH = 2
DEC_BATCH = 8
DEC_SEQ = 1
PAST_LEN = 16384
PAGE_SIZE = 128

RET_HEADS = 8
RET_DK = 128
RET_DV = 128
ATT_HEADS = 8
ATT_KV_HEADS = 2
ATT_HEAD_DIM = 128
IDX_HEADS = 16
IDX_DIM = 128
IDX_ROPE_DIM = 64
TOPK_MAX = 256
Q_BLOCK = 128
HG_HEADS = 8
HG_DK = 128
HG_DV = 128
CHUNK = 64
D_FF = 11008
ROPE_THETA = 10000.0
EPS = 1e-6
NEG_BIG = -1e30

RET_QK_W = RET_HEADS * RET_DK
RET_W = RET_HEADS * RET_DV
ATT_W = ATT_HEADS * ATT_HEAD_DIM
KV_W = ATT_KV_HEADS * ATT_HEAD_DIM
HG_K_W = HG_HEADS * HG_DK
HG_W = HG_HEADS * HG_DV
IN_SPLITS = (RET_QK_W, RET_QK_W, RET_W, RET_W,
             ATT_W, KV_W, KV_W, IDX_HEADS * IDX_DIM, IDX_DIM, IDX_HEADS,
             HG_K_W, HG_K_W, HG_W, HG_W,
             D_MODEL, D_MODEL, D_MODEL)
N_IN = sum(IN_SPLITS)

kernel_name = 'hybrid_ret_dsa_hgrn2_macaron_step'


def rms_norm(x, g):
    xf = x.astype(jnp.float32)
    y = xf * lax.rsqrt(jnp.mean(xf * xf, axis=-1, keepdims=True) + EPS)
    return (y * g.astype(jnp.float32)).astype(x.dtype)


def layer_norm(x, g, b):
    xf = x.astype(jnp.float32)
    mu = jnp.mean(xf, axis=-1, keepdims=True)
    var = jnp.mean(jnp.square(xf - mu), axis=-1, keepdims=True)
    y = (xf - mu) * lax.rsqrt(var + EPS)
    return (y * g.astype(jnp.float32) + b.astype(jnp.float32)).astype(x.dtype)


def split_cols(z):
    cuts = [int(c) for c in np.cumsum(IN_SPLITS)[:-1]]
    return jnp.split(z, cuts, axis=-1)


def rope_freqs(dim):
    return ROPE_THETA ** (-jnp.arange(0, dim, 2, dtype=jnp.float32) / dim)


def retnet_freqs(dim):
    return 1.0 / (ROPE_THETA ** jnp.linspace(0.0, 1.0, dim // 2, dtype=jnp.float32))


def apply_rotary(x, pos, freqs):
    half = x.shape[-1] // 2
    ang = pos.astype(jnp.float32)[:, None] * freqs[None, :]
    cos = jnp.cos(ang)[:, None, :]
    sin = jnp.sin(ang)[:, None, :]
    xf = x.astype(jnp.float32)
    x1, x2 = xf[..., :half], xf[..., half:]
    return jnp.concatenate([x1 * cos - x2 * sin, x2 * cos + x1 * sin], axis=-1).astype(x.dtype)


def swiglu(h, w1, w3, w2):
    return (jax.nn.silu(h @ w1) * (h @ w3)) @ w2


def chunk_size(t):
    return CHUNK if t % CHUNK == 0 else t


def to_chunks(a, c):
    b, t, h, d = a.shape
    return a.reshape(b, t // c, c, h, d).transpose(1, 0, 3, 2, 4)


def from_chunks(a):
    n, b, h, c, d = a.shape
    return a.transpose(1, 0, 3, 2, 4).reshape(b, n * c, h, d)


def retention_chunkwise(q, k, v, s0):
    f32 = jnp.float32
    c = chunk_size(q.shape[1])
    lg = jnp.log(1.0 - 2.0 ** (-5.0 - jnp.arange(RET_HEADS, dtype=f32)))
    tl = jnp.arange(c, dtype=f32)
    diff = tl[:, None] - tl[None, :]
    intra = jnp.where(diff >= 0, jnp.exp(lg[:, None, None] * jnp.maximum(diff, 0.0)), 0.0)
    q_dec = jnp.exp(lg[:, None] * (tl[None, :] + 1.0))[..., None]
    k_dec = jnp.exp(lg[:, None] * (c - 1.0 - tl[None, :]))[..., None]
    s_dec = jnp.exp(lg * c)[:, None, None]

    def step(s, inp):
        qc, kc, vc = inp
        a = jnp.einsum('bhtd,bhsd->bhts', qc, kc) * intra
        o = jnp.einsum('bhts,bhsv->bhtv', a, vc) + jnp.einsum('bhtd,bhdv->bhtv', qc * q_dec, s)
        s = s * s_dec + jnp.einsum('bhsd,bhsv->bhdv', kc * k_dec, vc)
        return s, o

    s, o = lax.scan(step, s0.astype(f32),
                    (to_chunks(q.astype(f32), c), to_chunks(k.astype(f32), c), to_chunks(v.astype(f32), c)))
    return from_chunks(o), s


def gla_chunkwise(q, k, v, log_f, s0):
    f32 = jnp.float32
    c = chunk_size(q.shape[1])
    causal = (jnp.arange(c)[:, None] >= jnp.arange(c)[None, :])[:, :, None]

    def step(s, inp):
        qc, kc, vc, gc = inp
        b = jnp.cumsum(gc, axis=2)
        b_last = b[:, :, -1, :]
        diff = b[:, :, :, None, :] - b[:, :, None, :, :]
        decay = jnp.where(causal, jnp.exp(jnp.minimum(diff, 0.0)), 0.0)
        a = jnp.einsum('bhtk,bhtsk,bhsk->bhts', qc, decay, kc)
        o = jnp.einsum('bhts,bhsv->bhtv', a, vc) + jnp.einsum('bhtk,bhkv->bhtv', qc * jnp.exp(b), s)
        s = s * jnp.exp(b_last)[..., None] + jnp.einsum(
            'bhsk,bhsv->bhkv', kc * jnp.exp(b_last[:, :, None, :] - b), vc)
        return s, o

    s, o = lax.scan(step, s0.astype(f32),
                    (to_chunks(q.astype(f32), c), to_chunks(k.astype(f32), c),
                     to_chunks(v.astype(f32), c), to_chunks(log_f.astype(f32), c)))
    return from_chunks(o), s


def dsa_attend(q, qi, wi, q_pos, k, v, ki, n_sel):
    b, t, h, d = q.shape
    n_kv = k.shape[2]
    key_pos = jnp.arange(k.shape[1], dtype=jnp.int32)
    visible = key_pos[None, :] <= q_pos[:, None]
    rel = jax.nn.relu(jnp.einsum('bthd,bsd->bths', qi, ki).astype(jnp.float32))
    score = jnp.einsum('bths,bth->bts', rel, wi.astype(jnp.float32))
    score = jnp.where(visible[None], score, NEG_BIG)
    _, sel = lax.top_k(score, n_sel)
    sel_ok = sel <= q_pos[None, :, None]
    take = jax.vmap(lambda arr, idx: arr[idx])
    k_sel = take(k, sel)
    v_sel = take(v, sel)
    qg = q.reshape(b, t, n_kv, h // n_kv, d)
    s = jnp.einsum('btngd,btsnd->btngs', qg, k_sel).astype(jnp.float32) * (d ** -0.5)
    s = jnp.where(sel_ok[:, :, None, None, :], s, NEG_BIG)
    p = jax.nn.softmax(s, axis=-1).astype(v.dtype)
    o = jnp.einsum('btngs,btsnd->btngd', p, v_sel)
    return o.reshape(b, t, h * d)


def dsa_sparse_attention(q, qi, wi, q_pos, k, v, ki):
    n_sel = min(TOPK_MAX, k.shape[1] // 4)
    b, t = q.shape[0], q.shape[1]
    if t % Q_BLOCK != 0:
        return dsa_attend(q, qi, wi, q_pos, k, v, ki, n_sel)
    nb = t // Q_BLOCK

    def blocks(a):
        return a.reshape((b, nb, Q_BLOCK) + a.shape[2:]).swapaxes(0, 1)

    out = lax.map(lambda xs: dsa_attend(xs[0], xs[1], xs[2], xs[3], k, v, ki, n_sel),
                  (blocks(q), blocks(qi), blocks(wi), q_pos.reshape(nb, Q_BLOCK)))
    return out.swapaxes(0, 1).reshape(b, t, -1)


def gather_pages(pool, page_table):
    g = pool[page_table]
    return g.reshape((g.shape[0], g.shape[1] * g.shape[2]) + g.shape[3:])


def token_mixing(h, pos, ret_state, hg_state, past_k, past_v, past_ik, lb,
                 w_in, ret_norm, q_norm, k_norm, idx_k_g, idx_k_b, hg_norm,
                 w_up_ret, w_up_att, w_up_hg, w_out):
    b, t, _ = h.shape
    f32 = jnp.float32
    (r_q, r_k, r_v, r_g, a_q, a_k, a_v, i_q, i_k, i_w,
     h_f, h_q, h_i, h_g, g_ret, g_att, g_hg) = split_cols(h @ w_in)

    rf = retnet_freqs(RET_DK)
    rq = apply_rotary(r_q.reshape(b, t, RET_HEADS, RET_DK), pos, rf)
    rk = apply_rotary(r_k.reshape(b, t, RET_HEADS, RET_DK), pos, rf) * (RET_DK ** -0.5)
    ro, ret_new = retention_chunkwise(rq, rk, r_v.reshape(b, t, RET_HEADS, RET_DV), ret_state)
    ro = rms_norm(ro, ret_norm.reshape(RET_HEADS, RET_DV)).reshape(b, t, RET_W)
    u_ret = (ro * jax.nn.silu(r_g.astype(f32))).astype(h.dtype) @ w_up_ret

    af = rope_freqs(ATT_HEAD_DIM)
    aq = apply_rotary(rms_norm(a_q.reshape(b, t, ATT_HEADS, ATT_HEAD_DIM), q_norm), pos, af)
    ak = apply_rotary(rms_norm(a_k.reshape(b, t, ATT_KV_HEADS, ATT_HEAD_DIM), k_norm), pos, af)
    av = a_v.reshape(b, t, ATT_KV_HEADS, ATT_HEAD_DIM)
    xf = rope_freqs(IDX_ROPE_DIM)
    iq = i_q.reshape(b, t, IDX_HEADS, IDX_DIM)
    iq = jnp.concatenate([apply_rotary(iq[..., :IDX_ROPE_DIM], pos, xf), iq[..., IDX_ROPE_DIM:]], axis=-1)
    ik = layer_norm(i_k, idx_k_g, idx_k_b)
    ik = jnp.concatenate([apply_rotary(ik[:, :, None, :IDX_ROPE_DIM], pos, xf)[:, :, 0],
                          ik[..., IDX_ROPE_DIM:]], axis=-1)
    if past_k is None:
        k_all, v_all, ik_all = ak, av, ik
    else:
        k_all = jnp.concatenate([past_k, ak], axis=1)
        v_all = jnp.concatenate([past_v, av], axis=1)
        ik_all = jnp.concatenate([past_ik, ik], axis=1)
    ao = dsa_sparse_attention(aq, iq * (IDX_DIM ** -0.5), i_w * (IDX_HEADS ** -0.5), pos,
                              k_all, v_all, ik_all)
    u_att = ao @ w_up_att

    fa = h_f.astype(f32)
    log_f = jax.nn.log_sigmoid(fa) + jnp.log1p(lb * jnp.exp(-fa))
    hk = (1.0 - lb) * jax.nn.sigmoid(-fa)
    hq = jax.nn.silu(h_q.astype(f32))
    shp = (b, t, HG_HEADS, HG_DK)
    ho, hg_new = gla_chunkwise(hq.reshape(shp), hk.reshape(shp),
                               h_i.reshape(b, t, HG_HEADS, HG_DV), log_f.reshape(shp), hg_state)
    ho = rms_norm(ho, hg_norm.reshape(HG_HEADS, HG_DV)).reshape(b, t, HG_W)
    u_hg = (ho * jax.nn.sigmoid(h_g.astype(f32))).astype(h.dtype) @ w_up_hg

    merged = jax.nn.sigmoid(g_ret) * u_ret + jax.nn.sigmoid(g_att) * u_att + jax.nn.sigmoid(g_hg) * u_hg
    return merged @ w_out, ret_new, hg_new, ak, av, ik


def decoder_layer(x, pos, ret_state, hg_state, past_k, past_v, past_ik, lb, lw):
    (ffn1_norm, ffn1_w1, ffn1_w3, ffn1_w2, mix_norm, w_in, ret_norm, q_norm, k_norm,
     idx_k_g, idx_k_b, hg_norm, w_up_ret, w_up_att, w_up_hg, w_out,
     ffn2_norm, ffn2_w1, ffn2_w3, ffn2_w2) = lw
    x = x + 0.5 * swiglu(rms_norm(x, ffn1_norm), ffn1_w1, ffn1_w3, ffn1_w2)
    y, ret_new, hg_new, k_new, v_new, ik_new = token_mixing(
        rms_norm(x, mix_norm), pos, ret_state, hg_state, past_k, past_v, past_ik, lb,
        w_in, ret_norm, q_norm, k_norm, idx_k_g, idx_k_b, hg_norm,
        w_up_ret, w_up_att, w_up_hg, w_out)
    x = x + y
    x = x + 0.5 * swiglu(rms_norm(x, ffn2_norm), ffn2_w1, ffn2_w3, ffn2_w2)
    return x, ret_new, hg_new, k_new, v_new, ik_new


def setup_inputs(seed: int = 0) -> dict:
    key = jax.random.key(seed)
    keys = iter(jax.random.split(key, 48))
    f32 = jnp.float32

    def nrm(shape, scale):
        return jax.random.normal(next(keys), shape, f32) * scale

    def gain(shape):
        return 1.0 + nrm(shape, 0.02)

    n_pages = PAST_LEN // PAGE_SIZE
    n_used = DEC_BATCH * n_pages
    n_pool = n_used + max(1, n_used // 4)
    page_table = jax.random.permutation(next(keys), n_pool)[:n_used].reshape(DEC_BATCH, n_pages).astype(jnp.int32)
    d = D_MODEL
    return {
        'x_prompt': nrm((BATCH, SEQ, d), 1.0),
        'x_sample': nrm((DEC_BATCH, DEC_SEQ, d), 1.0),
        'state_ret': nrm((DEPTH, DEC_BATCH, RET_HEADS, RET_DK, RET_DV), 1.0),
        'state_hgrn': nrm((DEPTH, DEC_BATCH, HG_HEADS, HG_DK, HG_DV), 0.5),
        'cache_k': nrm((DEPTH, n_pool, PAGE_SIZE, ATT_KV_HEADS, ATT_HEAD_DIM), 1.0),
        'cache_v': nrm((DEPTH, n_pool, PAGE_SIZE, ATT_KV_HEADS, ATT_HEAD_DIM), 1.0),
        'cache_idx_k': nrm((DEPTH, n_pool, PAGE_SIZE, IDX_DIM), 1.0),
        'page_table': page_table,
        'ffn1_norm': gain((DEPTH, d)),
        'ffn1_w1': nrm((DEPTH, d, D_FF), d ** -0.5),
        'ffn1_w3': nrm((DEPTH, d, D_FF), d ** -0.5),
        'ffn1_w2': nrm((DEPTH, D_FF, d), D_FF ** -0.5),
        'mix_norm': gain((DEPTH, d)),
        'w_in': nrm((DEPTH, d, N_IN), d ** -0.5),
        'ret_norm': gain((DEPTH, RET_W)),
        'q_norm': gain((DEPTH, ATT_HEAD_DIM)),
        'k_norm': gain((DEPTH, ATT_HEAD_DIM)),
        'idx_k_g': gain((DEPTH, IDX_DIM)),
        'idx_k_b': nrm((DEPTH, IDX_DIM), 0.02),
        'hg_lb_raw': nrm((DEPTH, HG_K_W), 0.5),
        'hg_norm': gain((DEPTH, HG_W)),
        'w_up_ret': nrm((DEPTH, RET_W, d), RET_W ** -0.5),
        'w_up_att': nrm((DEPTH, ATT_W, d), ATT_W ** -0.5),
        'w_up_hg': nrm((DEPTH, HG_W, d), HG_W ** -0.5),
        'w_out': nrm((DEPTH, d, d), d ** -0.5),
        'ffn2_norm': gain((DEPTH, d)),
        'ffn2_w1': nrm((DEPTH, d, D_FF), d ** -0.5),
        'ffn2_w3': nrm((DEPTH, d, D_FF), d ** -0.5),
        'ffn2_w2': nrm((DEPTH, D_FF, d), D_FF ** -0.5),
    }


def reference(x_prompt, x_sample, state_ret, state_hgrn, cache_k, cache_v, cache_idx_k, page_table,
              ffn1_norm, ffn1_w1, ffn1_w3, ffn1_w2, mix_norm, w_in, ret_norm, q_norm, k_norm,
              idx_k_g, idx_k_b, hg_lb_raw, hg_norm, w_up_ret, w_up_att, w_up_hg, w_out,
              ffn2_norm, ffn2_w1, ffn2_w3, ffn2_w2):
    pos_p = jnp.arange(x_prompt.shape[1], dtype=jnp.int32)
    pos_s = PAST_LEN + jnp.arange(x_sample.shape[1], dtype=jnp.int32)
    lb_soft = jax.nn.softmax(hg_lb_raw.astype(jnp.float32), axis=0)
    lb_all = jnp.cumsum(lb_soft, axis=0) - lb_soft[0]
    bp = x_prompt.shape[0]
    xp, xs = x_prompt, x_sample
    rp_l, rs_l, hp_l, hs_l = [], [], [], []
    kp_l, vp_l, ip_l, ks_l, vs_l, is_l = [], [], [], [], [], []
    for l in range(DEPTH):
        lw = (ffn1_norm[l], ffn1_w1[l], ffn1_w3[l], ffn1_w2[l], mix_norm[l], w_in[l], ret_norm[l],
              q_norm[l], k_norm[l], idx_k_g[l], idx_k_b[l], hg_norm[l], w_up_ret[l], w_up_att[l],
              w_up_hg[l], w_out[l], ffn2_norm[l], ffn2_w1[l], ffn2_w3[l], ffn2_w2[l])
        zr = jnp.zeros((bp, RET_HEADS, RET_DK, RET_DV), jnp.float32)
        zh = jnp.zeros((bp, HG_HEADS, HG_DK, HG_DV), jnp.float32)
        xp, r_p, h_p, k_p, v_p, i_p = decoder_layer(xp, pos_p, zr, zh, None, None, None, lb_all[l], lw)
        past_k = gather_pages(cache_k[l], page_table)
        past_v = gather_pages(cache_v[l], page_table)
        past_ik = gather_pages(cache_idx_k[l], page_table)
        xs, r_s, h_s, k_s, v_s, i_s = decoder_layer(xs, pos_s, state_ret[l], state_hgrn[l],
                                                    past_k, past_v, past_ik, lb_all[l], lw)
        rp_l.append(r_p); rs_l.append(r_s); hp_l.append(h_p); hs_l.append(h_s)
        kp_l.append(k_p); vp_l.append(v_p); ip_l.append(i_p)
        ks_l.append(k_s); vs_l.append(v_s); is_l.append(i_s)
    return (xp, xs,
            jnp.stack(rp_l).astype(state_ret.dtype), jnp.stack(rs_l).astype(state_ret.dtype),
            jnp.stack(hp_l).astype(state_hgrn.dtype), jnp.stack(hs_l).astype(state_hgrn.dtype),
            jnp.stack(kp_l), jnp.stack(vp_l), jnp.stack(ip_l),
            jnp.stack(ks_l), jnp.stack(vs_l), jnp.stack(is_l))
```

```python
import numpy as np
from contextlib import ExitStack
import concourse.bass as bass
import concourse.mybir as mybir
from concourse.bass_utils import run_bass_kernel_spmd

F32 = mybir.dt.float32
BF16 = mybir.dt.bfloat16
I32 = mybir.dt.int32
AF = mybir.ActivationFunctionType
ALU = mybir.AluOpType
AX = mybir.AxisListType
EPS = 1e-6


import types


def _freeze(fn):
    if fn is None or fn.__closure__ is None:
        return fn
    cells = []
    for c in fn.__closure__:
        try:
            cells.append(types.CellType(c.cell_contents))
        except ValueError:
            cells.append(c)
    return types.FunctionType(fn.__code__, fn.__globals__, fn.__name__, fn.__defaults__, tuple(cells))


class Prog:
    ENGS = ("pe", "act", "dve", "pool", "sp")

    def __init__(self, nc, stack):
        self.nc = nc
        self.stack = stack
        self.ops = {e: [] for e in self.ENGS}
        self.cnt = {e: 0 for e in self.ENGS}
        self.dcnt = {}
        self.lastw = {}
        self.readers = {}
        self.waited = {e: {} for e in self.ENGS}
        self.esem = {e: stack.enter_context(nc.semaphore("s_" + e)) for e in self.ENGS}
        self.dsem = {}

    def sb(self, name, shape, dtype):
        return self.stack.enter_context(self.nc.sbuf_tensor(name, list(shape), dtype))

    def ps(self, name, shape, dtype=F32):
        return self.stack.enter_context(self.nc.psum_tensor(name, list(shape), dtype))

    def op(self, eng, fn, reads=(), writes=(), dma=None):
        deps = set()
        for r in reads:
            if r in self.lastw:
                deps.add(self.lastw[r])
        for w in writes:
            if w in self.lastw:
                deps.add(self.lastw[w])
            for ev in self.readers.get(w, ()):
                deps.add(ev)
        best = {}
        for ev in deps:
            k = (ev[0], ev[1])
            v = self.dcnt[ev[1]] if ev[0] == "d" else ev[2]
            if v > best.get(k, 0):
                best[k] = v
        waits = []
        wd = self.waited[eng]
        for k, v in best.items():
            if wd.get(k, 0) >= v:
                continue
            wd[k] = v
            waits.append((k, v))
        if dma is not None:
            if dma not in self.dsem:
                self.dsem[dma] = self.stack.enter_context(self.nc.semaphore("d_" + dma))
                self.dcnt[dma] = 0
            self.dcnt[dma] += 16
            ev = ("d", dma, self.dcnt[dma])
        else:
            self.cnt[eng] += 1
            ev = ("e", eng, self.cnt[eng])
        for w in writes:
            self.lastw[w] = ev
            self.readers[w] = []
        for r in reads:
            if r not in writes:
                self.readers.setdefault(r, []).append(ev)
        self.ops[eng].append((waits, _freeze(fn), ev))

    def barrier(self):
        for e in self.ENGS:
            waits = []
            wd = self.waited[e]
            for k, v in self.dcnt.items():
                if wd.get(("d", k), 0) < v:
                    wd[("d", k)] = v
                    waits.append((("d", k), v))
            for e2, v in self.cnt.items():
                if v > 0 and wd.get(("e", e2), 0) < v:
                    wd[("e", e2)] = v
                    waits.append((("e", e2), v))
            if waits:
                self.ops[e].append((waits, None, None))
        self.lastw = {}
        self.readers = {}

    def finish(self):
        waits = [(("d", k), v) for k, v in self.dcnt.items()]
        waits += [(("e", e), v) for e, v in self.cnt.items() if v > 0 and e != "sp"]
        self.ops["sp"].append((waits, None, None))

    def emit(self):
        nc = self.nc
        with nc.Block() as block:
            def run(eng_name):
                def body(eng):
                    for waits, fn, ev in self.ops[eng_name]:
                        for k, v in waits:
                            sem = self.esem[k[1]] if k[0] == "e" else self.dsem[k[1]]
                            eng.wait_ge(sem, v)
                        if fn is None:
                            continue
                        ins = fn(eng)
                        if ev[0] == "e":
                            ins.then_inc(self.esem[ev[1]], 1)
                        else:
                            ins.then_inc(self.dsem[ev[1]], 16)
                return body
            block.tensor(run("pe"))
            block.scalar(run("act"))
            block.vector(run("dve"))
            block.gpsimd(run("pool"))
            block.sync(run("sp"))


FULL = dict(D=4096, DFF=11008, NT=2048, T=512, L=2, NPAST=16384, NPOOL=1280, TOPK=256)


class Builder:
    def __init__(self, cfg, stages=("ffn1", "mix", "ffn2")):
        self.cfg = cfg
        self.stages = stages
        c = cfg
        self.D, self.DFF, self.NT, self.T, self.L = c["D"], c["DFF"], c["NT"], c["T"], c["L"]
        self.KD = self.D // 128
        self.KF = self.DFF // 128
        self.NTOT = self.NT + 1
        self.nc = bass.Bass("TRN2", target_bir_lowering=False)
        self.nc.allow_low_precision("bf16 matmul operands, fp32 accumulate")

    def din(self, name, shape, dt=F32):
        return self.nc.dram_tensor(name, list(shape), dt, kind="ExternalInput").ap()

    def dout(self, name, shape, dt=F32):
        return self.nc.dram_tensor(name, list(shape), dt, kind="ExternalOutput").ap()

    def build(self):
        nc = self.nc
        D, DFF, NT, T, L, KD, KF = self.D, self.DFF, self.NT, self.T, self.L, self.KD, self.KF
        self.x_p = self.din("x_p", [NT, D])
        self.x_s = self.din("x_s", [1, D])
        self.w = {}
        self.KG2 = 43 if KF % 43 == 0 else KF
        while self.KG2 * 128 > 8192:
            self.KG2 //= 2
        assert KF % self.KG2 == 0 and DFF % 256 == 0 and D % 256 == 0
        ng = KF // self.KG2
        for nm, shp in (("ffn1_norm", [L, D]), ("mix_norm", [L, D]), ("ffn2_norm", [L, D])):
            self.w[nm] = self.din(nm, shp)
        for f in ("ffn1", "ffn2"):
            self.w[f + "_w13"] = self.din(f + "_w13", [L, DFF // 128, 128, 2 * KD, 128])
            self.w[f + "_w2"] = self.din(f + "_w2", [L, KD, ng, 128, self.KG2, 128])
        if "mix" in self.stages:
            _mixer_decl(self)
        self.y_p = self.dout("y_p", [NT, D])
        self.y_s = self.dout("y_s", [1, D])
        self.xT = nc.dram_tensor("xT_scr", [128, KD, self.NTOT], F32, kind="Internal").ap()
        with ExitStack() as st:
            P = Prog(nc, st)
            self.P = P
            self.alloc()
            if "mix" in self.stages:
                _mixer_alloc(self)
            self.consts()
            self.tiles = [(t0, T) for t0 in range(0, NT, T)] + [(NT, 1)]
            for ti, (t0, tn) in enumerate(self.tiles):
                self.load_x(ti, t0, tn)
            for l in range(L):
                if "ffn1" in self.stages:
                    for ti, (t0, tn) in enumerate(self.tiles):
                        self.norm_hT(l, 0, t0, tn)
                        self.ffn(l, "ffn1", t0, tn)
                    P.barrier()
                if "mix" in self.stages:
                    for ti, (t0, tn) in enumerate(self.tiles):
                        self.norm_hT(l, 1, t0, tn, hf32=(self.hf32 if ti == 0 else None))
                        P.barrier()
                        self.mixer(l, ti, t0, tn)
                        P.barrier()
                if "ffn2" in self.stages:
                    for ti, (t0, tn) in enumerate(self.tiles):
                        self.norm_hT(l, 2, t0, tn)
                        self.ffn(l, "ffn2", t0, tn)
                    P.barrier()
            for ti, (t0, tn) in enumerate(self.tiles):
                self.store_x(ti, t0, tn)
            P.finish()
            P.emit()
        return nc

    def alloc(self):
        P, T, KD, KF = self.P, self.T, self.KD, self.KF
        self.hT = P.sb("hT", [128, KD, T], BF16)
        self.AR = max(KF * T, 53248 if "mix" in self.stages else 0)
        self.arena = P.sb("arena", [128, self.AR], BF16)
        self.SLOT = 8192
        self.NSLOT = 3
        self.slots = [P.sb("ws%d" % i, [128, self.SLOT], BF16) for i in range(self.NSLOT)]
        self.slot_i = 0
        self.xc = [P.sb("xc%d" % i, [128, T], F32) for i in range(2)]
        self.yv = [P.sb("yv%d" % i, [128, max(T, 512, 1024 if self.cfg.get("debug") else 0)], F32) for i in range(2)]
        self.sq = [P.sb("sq%d" % i, [128, max(T, 512)], F32) for i in range(2)]
        self.rstd = P.sb("rstd", [128, T], F32)
        self.ident = P.sb("ident", [128, 128], F32)
        self.ones = P.sb("ones", [128, 128], F32)
        self.gains = P.sb("gains", [128, self.L * 3 * KD], F32)
        self.xio = P.sb("xio", [128, 512], F32)
        self.pb = [P.ps("pb%d" % i, [128, 512], F32) for i in range(7)] + [None]
        pb7 = P.ps("pb7", [128, 512], F32)
        self.pb[7] = pb7
        self.pbT = pb7[:, 256:512].bitcast(BF16)
        self.slab_i = 0
        self.ctr = 0

    def uid(self):
        self.ctr += 1
        return self.ctr

    def consts(self):
        P, nc = self.P, self.nc
        ident, ones = self.ident, self.ones
        P.op("pool", lambda e: e.memset(ident[:], 1.0), writes=["ident"])
        P.op("pool", lambda e: e.affine_select(out=ident[:], in_=ident[:], pattern=[[-1, 128]],
                                               compare_op=ALU.is_equal, fill=0.0, base=0, channel_multiplier=1),
             reads=["ident"], writes=["ident"])
        P.op("pool", lambda e: e.memset(ones[:], 1.0), writes=["ones"])
        KD, L = self.KD, self.L
        gains = self.gains
        with nc.allow_non_contiguous_dma(reason="tiny gain vectors to feature-major"):
            for wi, nm in enumerate(("ffn1_norm", "mix_norm", "ffn2_norm")):
                if nm not in self.w:
                    continue
                for l in range(L):
                    o = (l * 3 + wi) * KD
                    src = self.w[nm][l].rearrange("(c p) -> p c", p=128)
                    P.op("sp", lambda e, o=o, src=src: e.dma_start(out=gains[:, o:o + KD], in_=src, allow_slow_non_contiguous=True),
                         writes=["gains"], dma="gains")

    def load_x(self, ti, t0, tn):
        P, KD = self.P, self.KD
        src_all = self.x_p if t0 < self.NT else self.x_s
        r0 = t0 if t0 < self.NT else 0
        xio, ident, pb = self.xio, self.ident, self.pb
        for b0 in range(0, tn, 128):
            bn = min(128, tn - b0)
            for c0 in range(0, KD, 4):
                cn = min(4, KD - c0)
                src = src_all[r0 + b0:r0 + b0 + bn, c0 * 128:(c0 + cn) * 128]
                P.op("sp", lambda e, src=src, bn=bn, cn=cn: e.dma_start(out=xio[:bn, :cn * 128], in_=src),
                     writes=["xio"], dma="xio")
                bank = pb[7]

                def tr(e, bn=bn, cn=cn, bank=bank):
                    for c in range(cn):
                        ins = e.transpose(bank[:, c * bn:(c + 1) * bn], xio[:bn, c * 128:(c + 1) * 128], ident[:bn, :bn])
                    return ins
                P.op("pe", tr, reads=["xio", "ident"], writes=["pb7"])
                yv = self.yv[0]
                P.op("dve", lambda e, bn=bn, cn=cn, bank=bank, yv=yv: e.tensor_copy(yv[:, :cn * bn], bank[:, :cn * bn]),
                     reads=["pb7"], writes=["yv0"])
                dst = self.xT[:, c0:c0 + cn, t0 + b0:t0 + b0 + bn]
                P.op("sp", lambda e, dst=dst, bn=bn, cn=cn, yv=yv: e.dma_start(
                    out=dst, in_=yv[:, :cn * bn].rearrange("p (c t) -> p c t", t=bn), allow_slow_non_contiguous=(bn < 128)),
                    reads=["yv0"], writes=["xT%d" % c for c in range(c0, c0 + cn)], dma="xTw")

    def store_x(self, ti, t0, tn):
        P, KD = self.P, self.KD
        dst_all = self.y_p if t0 < self.NT else self.y_s
        r0 = t0 if t0 < self.NT else 0
        xio, ident, pb = self.xio, self.ident, self.pb
        for b0 in range(0, tn, 128):
            bn = min(128, tn - b0)
            for c0 in range(0, KD, 4):
                cn = min(4, KD - c0)
                yv = self.yv[0]
                src = self.xT[:, c0:c0 + cn, t0 + b0:t0 + b0 + bn]
                P.op("sp", lambda e, src=src, bn=bn, cn=cn, yv=yv: e.dma_start(
                    out=yv[:, :cn * bn].rearrange("p (c t) -> p c t", t=bn), in_=src, allow_slow_non_contiguous=(bn < 128)),
                    reads=["xT%d" % c for c in range(c0, c0 + cn)], writes=["yv0"], dma="yv0")
                bank = pb[7]

                def tr(e, bn=bn, cn=cn, bank=bank, yv=yv):
                    for c in range(cn):
                        ins = e.transpose(bank[:bn, c * 128:(c + 1) * 128], yv[:, c * bn:(c + 1) * bn], ident[:, :])
                    return ins
                P.op("pe", tr, reads=["yv0", "ident"], writes=["pb7"])
                P.op("dve", lambda e, bn=bn, cn=cn, bank=bank: e.tensor_copy(xio[:bn, :cn * 128], bank[:bn, :cn * 128]),
                     reads=["pb7"], writes=["xio"])
                dst = dst_all[r0 + b0:r0 + b0 + bn, c0 * 128:(c0 + cn) * 128]
                P.op("sp", lambda e, dst=dst, bn=bn, cn=cn: e.dma_start(out=dst, in_=xio[:bn, :cn * 128]),
                     reads=["xio"], dma="yout")

    def norm_hT(self, l, which, t0, tn, hf32=None):
        P, KD, D = self.P, self.KD, self.D
        bank = self.pb[6]
        ones = self.ones
        slow = tn < 128
        for c in range(KD):
            xc, kx = self.xc[c % 2], "xc%d" % (c % 2)
            src = self.xT[:, c, t0:t0 + tn]
            P.op("sp", lambda e, xc=xc, src=src: e.dma_start(out=xc[:, :tn], in_=src, allow_slow_non_contiguous=slow),
                 reads=["xT%d" % c], writes=[kx], dma=kx)
            sq = self.sq[c % 2]
            P.op("act", lambda e, xc=xc, sq=sq: e.activation(out=sq[:, :tn], in_=xc[:, :tn], func=AF.Square),
                 reads=[kx], writes=["sq%d" % (c % 2)])
            P.op("pe", lambda e, c=c, sq=sq: e.matmul(bank[:, :tn], lhsT=ones[:], rhs=sq[:, :tn],
                                                     start=(c == 0), stop=(c == KD - 1)),
                 reads=["sq%d" % (c % 2), "ones"], writes=["pb6"])
        rstd = self.rstd
        P.op("dve", lambda e: e.tensor_scalar(out=rstd[:, :tn], in0=bank[:, :tn], scalar1=1.0 / D, scalar2=EPS,
                                              op0=ALU.mult, op1=ALU.add), reads=["pb6"], writes=["rstd"])
        P.op("act", lambda e: e.sqrt(out=rstd[:, :tn], in_=rstd[:, :tn]), reads=["rstd"], writes=["rstd"])
        P.op("dve", lambda e: e.reciprocal(out=rstd[:, :tn], in_=rstd[:, :tn]), reads=["rstd"], writes=["rstd"])
        gains, hT = self.gains, self.hT
        go = (l * 3 + which) * KD
        for c in range(KD):
            xc, kx = self.xc[c % 2], "xc%d" % (c % 2)
            src = self.xT[:, c, t0:t0 + tn]
            P.op("sp", lambda e, xc=xc, src=src: e.dma_start(out=xc[:, :tn], in_=src, allow_slow_non_contiguous=slow),
                 reads=["xT%d" % c], writes=[kx], dma=kx)
            P.op("dve", lambda e, c=c, xc=xc: e.scalar_tensor_tensor(out=hT[:, c, :tn], in0=xc[:, :tn],
                                                                     scalar=gains[:, go + c:go + c + 1], in1=rstd[:, :tn],
                                                                     op0=ALU.mult, op1=ALU.mult),
                 reads=[kx, "gains", "rstd"], writes=["hT"])
            if hf32 is not None:
                P.op("dve", lambda e, c=c, xc=xc: e.scalar_tensor_tensor(out=hf32[:, c, :], in0=xc[:, :128],
                                                                         scalar=gains[:, go + c:go + c + 1], in1=rstd[:, :128],
                                                                         op0=ALU.mult, op1=ALU.mult),
                     reads=[kx, "gains", "rstd"], writes=["hf32"])

    def wload(self, src, nk, ncol):
        P = self.P
        s = self.slot_i % self.NSLOT
        self.slot_i += 1
        view = self.slots[s][:, :nk * ncol].rearrange("p (k c) -> p k c", c=ncol)
        key = "ws%d" % s
        P.op("pool", lambda e: e.dma_start(out=view, in_=src), writes=[key], dma=key)
        return view, key

    def ffn(self, l, name, t0, tn):
        P, KD, KF = self.P, self.KD, self.KF
        w13, w2 = self.w[name + "_w13"][l], self.w[name + "_w2"][l]
        hT, pb = self.hT, self.pb
        gT = self.arena[:, :KF * tn].rearrange("p (c t) -> p c t", t=tn)
        for j in range(KF):
            v, kw = self.wload(w13[j], 2 * KD, 128)
            ba, bb = pb[j % 2], pb[2 + j % 2]
            ka, kb = "pb%d" % (j % 2), "pb%d" % (2 + j % 2)

            def mm(e, v=v, bank=ba):
                for k in range(KD):
                    ins = e.matmul(bank[:, :tn], lhsT=v[:, k, :], rhs=hT[:, k, :tn], start=(k == 0), stop=(k == KD - 1))
                return ins
            P.op("pe", mm, reads=[kw, "hT"], writes=[ka])

            def mm3(e, v=v, bank=bb):
                for k in range(KD):
                    ins = e.matmul(bank[:, :tn], lhsT=v[:, KD + k, :], rhs=hT[:, k, :tn], start=(k == 0), stop=(k == KD - 1))
                return ins
            P.op("pe", mm3, reads=[kw, "hT"], writes=[kb])
            sq = self.sq[j % 2]
            P.op("act", lambda e, sq=sq, ba=ba: e.activation(out=sq[:, :tn], in_=ba[:, :tn], func=AF.Silu),
                 reads=[ka], writes=["sq%d" % (j % 2)])
            P.op("dve", lambda e, sq=sq, bb=bb, j=j: e.tensor_tensor(out=gT[:, j, :], in0=sq[:, :tn], in1=bb[:, :tn], op=ALU.mult),
                 reads=["sq%d" % (j % 2), kb], writes=["arena"])
        KG = self.KG2
        ngr = KF // KG
        for fo in range(KD):
            views = []
            for g in range(ngr):
                kn = min(KG, KF - g * KG)
                v, k = self.wload(w2[fo, g], kn, 128)
                views.append((v, k, g * KG, kn))
            by = pb[4 + fo % 2]
            ky = "pb%d" % (4 + fo % 2)

            def mmd(e, views=views, by=by):
                n = 0
                for v, k, kk0, kn in views:
                    for kk in range(kn):
                        ins = e.matmul(by[:, :tn], lhsT=v[:, kk, :], rhs=gT[:, kk0 + kk, :],
                                       start=(n == 0), stop=(n == KF - 1))
                        n += 1
                return ins
            P.op("pe", mmd, reads=[k for _, k, _, _ in views] + ["arena"], writes=[ky])
            xc = self.xc[fo % 2]
            kx = "xc%d" % (fo % 2)
            src = self.xT[:, fo, t0:t0 + tn]
            P.op("sp", lambda e, xc=xc, src=src: e.dma_start(out=xc[:, :tn], in_=src, allow_slow_non_contiguous=(tn < 128)), reads=["xT%d" % fo], writes=[kx], dma=kx)
            P.op("dve", lambda e, xc=xc, by=by: e.scalar_tensor_tensor(out=xc[:, :tn], in0=by[:, :tn], scalar=0.5,
                                                                      in1=xc[:, :tn], op0=ALU.mult, op1=ALU.add),
                 reads=[ky, kx], writes=[kx])
            P.op("sp", lambda e, xc=xc, src=src: e.dma_start(out=src, in_=xc[:, :tn], allow_slow_non_contiguous=(tn < 128)), reads=[kx], writes=["xT%d" % fo], dma=kx + "s")


def run(cfg, inputs_per_core, stages=("ffn1", "mix", "ffn2"), n_cores=8):
    b = Builder(cfg, stages)
    nc = b.build()
    res = run_bass_kernel_spmd(nc, inputs_per_core, core_ids=list(range(n_cores)))
    return res.results


def _blk(w, kc, cw):
    L, K, N = w.shape
    return np.ascontiguousarray(w.reshape(L, K // 128, 128, N // cw, cw).transpose(0, 3, 2, 1, 4))


def make_maps(cfg, inp, stages):
    D, DFF, L = cfg["D"], cfg["DFF"], cfg["L"]
    KF = DFF // 128
    KG = 43 if KF % 43 == 0 else KF
    while KG * 128 > 8192:
        KG //= 2
    shared = {k: np.ascontiguousarray(inp[k]) for k in ("ffn1_norm", "mix_norm", "ffn2_norm")}
    for f in ("ffn1", "ffn2"):
        b13 = np.stack([_blk(inp[f + "_w1"], D // 128, 128), _blk(inp[f + "_w3"], D // 128, 128)], axis=3)
        shared[f + "_w13"] = b13.reshape(L, DFF // 128, 128, 2 * (D // 128), 128)
        w2 = np.asarray(inp[f + "_w2"])
        shared[f + "_w2"] = np.ascontiguousarray(
            w2.reshape(L, KF // KG, KG, 128, D // 128, 128).transpose(0, 4, 1, 3, 2, 5))
    if "mix" in stages:
        for k in ("ret_norm", "hg_norm", "hg_lb_raw", "q_norm", "k_norm", "idx_k_g", "idx_k_b", "cache_k", "cache_v", "cache_idx_k"):
            shared[k] = np.ascontiguousarray(inp[k])
        win = np.asarray(inp["w_in"])
        fm, tm = win_blocks(D)
        sec, _ = _sections(D)
        KD = D // 128
        fmw = np.stack([win[:, :, o:o + 128] for o in fm], axis=1)
        shared["w_in_fm"] = np.ascontiguousarray(fmw.reshape(L, len(fm), KD, 128, 128).transpose(0, 1, 3, 2, 4))
        tmw = np.stack([win[:, :, o:o + 256] for o in tm], axis=1)
        shared["w_in_tm"] = np.ascontiguousarray(tmw.reshape(L, len(tm), KD, 128, 256).transpose(0, 1, 3, 2, 4))
        o = sec["i_w"][0]
        shared["w_in_iw"] = np.ascontiguousarray(win[:, :, o:o + 16].reshape(L, KD, 128, 16).transpose(0, 2, 1, 3))
        for k in ("w_up_ret", "w_up_att", "w_up_hg"):
            shared[k] = _blk(inp[k], 8, 128)
        shared["w_out"] = _blk(inp["w_out"], KD, 256)
        ht = host_tables(cfg)
        shared.update(tabs=ht["tabs"], dec=ht["dec"], causT=ht["causT"], caus64=ht["caus64"], trineg=ht["trineg"], tri01=ht["tri01"])
    maps = []
    nseq = inp["x_prompt"].shape[0]
    for c in range(inp["x_sample"].shape[0]):
        m = dict(shared)
        m["x_p"] = np.ascontiguousarray(inp["x_prompt"][c % nseq])
        m["x_s"] = np.ascontiguousarray(inp["x_sample"][c])
        if "mix" in stages:
            m["state_ret"] = np.ascontiguousarray(inp["state_ret"][:, c])
            m["state_hgrn"] = np.ascontiguousarray(inp["state_hgrn"][:, c])
            m["pt"] = np.ascontiguousarray(inp["page_table"][c:c + 1]).astype(np.int32)
        maps.append(m)
    return maps


def gather_outputs(cfg, r, nseq, nb):
    st = lambda k, n, ax: np.stack([r[c][k] for c in range(n)], axis=ax)
    return (st("y_p", nseq, 0), st("y_s", nb, 0), st("rsp", nseq, 1), st("rss", nb, 1), st("hsp", nseq, 1), st("hss", nb, 1),
            st("kp", nseq, 1), st("vp", nseq, 1), st("ikp", nseq, 1), st("ks", nb, 1), st("vs", nb, 1), st("iks", nb, 1))


def kernel(**inp):
    cfg = FULL
    stages = ("ffn1", "mix", "ffn2")
    maps = make_maps(cfg, inp, stages)
    r = run(cfg, maps, stages)
    return gather_outputs(cfg, r, inp["x_prompt"].shape[0], inp["x_sample"].shape[0])


RH, RDK = 8, 128
SEC = {}


def _sections(D):
    names = ("r_q", "r_k", "r_v", "r_g", "a_q", "a_k", "a_v", "i_q", "i_k", "i_w",
             "h_f", "h_q", "h_i", "h_g", "g_ret", "g_att", "g_hg")
    sizes = (1024, 1024, 1024, 1024, 1024, 256, 256, 2048, 128, 16, 1024, 1024, 1024, 1024, D, D, D)
    o, d = 0, {}
    for n, z in zip(names, sizes):
        d[n] = (o, z)
        o += z
    return d, o


def host_tables(cfg):
    NT, NPAST = cfg["NT"], cfg["NPAST"]
    pos = np.concatenate([np.arange(NT), [NPAST]]).astype(np.float32)
    f_ret = (1.0 / (10000.0 ** np.linspace(0.0, 1.0, 64, dtype=np.float32))).astype(np.float32)
    f_att = (10000.0 ** (-np.arange(0, 128, 2, dtype=np.float32) / 128)).astype(np.float32)
    f_idx = (10000.0 ** (-np.arange(0, 64, 2, dtype=np.float32) / 64)).astype(np.float32)
    tabs = np.zeros((6, 128, NT + 1), np.float32)
    a = (pos[None, :] * f_ret[:, None]).astype(np.float32)
    tabs[0] = np.concatenate([np.cos(a), np.cos(a)], 0)
    tabs[1] = np.concatenate([np.sin(a), np.sin(a)], 0)
    a = (pos[None, :] * f_att[:, None]).astype(np.float32)
    tabs[2] = np.concatenate([np.cos(a), np.cos(a)], 0)
    tabs[3] = np.concatenate([np.sin(a), np.sin(a)], 0)
    a = (pos[None, :] * f_idx[:, None]).astype(np.float32)
    tabs[4] = np.concatenate([np.cos(a), np.cos(a), np.ones((64, NT + 1), np.float32)], 0)
    tabs[5] = np.concatenate([np.sin(a), np.sin(a), np.zeros((64, NT + 1), np.float32)], 0)
    lg = np.log(1.0 - 2.0 ** (-5.0 - np.arange(8, dtype=np.float64)))
    tl = np.arange(128, dtype=np.float64)
    dq = np.exp(lg[:, None] * (tl[None, :] + 1.0))
    dk = np.exp(-lg[:, None] * (tl[None, :] + 1.0)) * 128 ** -0.5
    dec = np.zeros((128, 2, 8, 128), np.float32)
    dec[:, 0] = dq[None].astype(np.float32)
    dec[:, 1] = dk[None].astype(np.float32)
    causT = (np.arange(128)[:, None] <= np.arange(128)[None, :]).astype(np.float32)
    ii = np.arange(128)
    caus64 = ((ii[:, None] <= ii[None, :]) & ((ii[:, None] // 64) == (ii[None, :] // 64))).astype(np.float32)
    trineg = np.where(ii[None, :] <= ii[:, None], 0.0, -1e30).astype(np.float32)
    return dict(tabs=tabs, dec=dec.reshape(128, 2 * 8 * 128), causT=causT, caus64=caus64, trineg=trineg, tri01=np.ascontiguousarray(causT.T),
                gamC=np.exp(lg * 128).astype(np.float32), gam1=np.exp(lg).astype(np.float32))


def win_blocks(D):
    sec, _ = _sections(D)
    fm = []
    for n in ("r_q", "r_k", "r_g", "a_q", "a_k", "i_q", "i_k", "h_f", "h_q", "h_g", "g_ret", "g_att", "g_hg"):
        fm += [sec[n][0] + j * 128 for j in range(sec[n][1] // 128)]
    tm = []
    for n in ("r_v", "h_i", "a_v"):
        tm += [sec[n][0] + j * 256 for j in range(sec[n][1] // 256)]
    return fm, tm


def _mixer_decl(self):
    D, L, NT = self.D, self.L, self.NT
    assert L == 2
    self.sec, self.NIN = _sections(D)
    KD = self.KD
    self.fm_offs, self.tm_offs = win_blocks(D)
    self.fm_idx = {o: i for i, o in enumerate(self.fm_offs)}
    self.tm_idx = {o: i for i, o in enumerate(self.tm_offs)}
    for nm, shp in (("w_in_fm", [L, len(self.fm_offs), 128, KD, 128]), ("w_in_tm", [L, len(self.tm_offs), 128, KD, 256]),
                    ("w_in_iw", [L, 128, KD, 16]), ("ret_norm", [L, 1024]), ("w_up_ret", [L, KD, 128, 8, 128]),
                    ("hg_norm", [L, 1024]), ("hg_lb_raw", [L, 1024]), ("w_up_hg", [L, KD, 128, 8, 128]),
                    ("w_out", [L, D // 256, 128, KD, 256]), ("state_ret", [L, 8, 128, 128]), ("state_hgrn", [L, 8, 128, 128]),
                    ("q_norm", [L, 128]), ("k_norm", [L, 128]), ("idx_k_g", [L, 128]), ("idx_k_b", [L, 128]),
                    ("w_up_att", [L, KD, 128, 8, 128]),
                    ("cache_k", [L, self.cfg["NPOOL"], 128, 2, 128]), ("cache_v", [L, self.cfg["NPOOL"], 128, 2, 128]),
                    ("cache_idx_k", [L, self.cfg["NPOOL"], 128, 128])):
        self.w[nm] = self.din(nm, shp)
    self.tabs = self.din("tabs", [6, 128, NT + 1])
    self.dec_d = self.din("dec", [128, 2048])
    self.causT_d = self.din("causT", [128, 128])
    self.caus64_d = self.din("caus64", [128, 128])
    self.rsp = self.dout("rsp", [L, 8, 128, 128])
    self.rss = self.dout("rss", [L, 8, 128, 128])
    self.hsp = self.dout("hsp", [L, 8, 128, 128])
    self.hss = self.dout("hss", [L, 8, 128, 128])
    self.pt_d = self.din("pt", [1, self.cfg["NPAST"] // 128], I32)
    self.trineg_d = self.din("trineg", [128, 128])
    self.tri01_d = self.din("tri01", [128, 128])
    self.kp = self.dout("kp", [L, NT, 2, 128])
    self.vp = self.dout("vp", [L, NT, 2, 128])
    self.ikp = self.dout("ikp", [L, NT, 128])
    self.ks = self.dout("ks", [L, 1, 2, 128])
    self.vs = self.dout("vs", [L, 1, 2, 128])
    self.iks = self.dout("iks", [L, 1, 128])
    self.NSEL_P = self.cfg.get("NSEL_P", min(256, NT // 4))
    self.NSEL_S = min(256, (self.cfg["NPAST"] + 1) // 4)
    ht = host_tables(self.cfg)
    self.gamC, self.gam1 = ht["gamC"], ht["gam1"]


def carve(self, n_bf16):
    o = self.ar_off
    self.ar_off += (n_bf16 + 15) // 16 * 16
    assert self.ar_off <= self.AR, (self.ar_off, self.AR)
    return self.arena[:, o:o + n_bf16]


def cf32(self, n):
    return carve(self, 2 * n).bitcast(F32)


def _mixer_alloc(self):
    P, T = self.P, self.T
    self.ar_off = 0
    self.m = {}
    m = self.m
    for nm in ("ret", "hg"):
        m["S_" + nm] = cf32(self, 1024).rearrange("p (h v) -> p h v", v=128)
        m["Sb_" + nm] = carve(self, 1024).rearrange("p (h v) -> p h v", v=128)
        m["g" + nm] = carve(self, 8 * T).rearrange("p (h t) -> p h t", t=T)
        m["nrm_" + nm] = cf32(self, 8)
    m["causT"] = carve(self, 128)
    m["caus64"] = carve(self, 128)
    m["lb"] = cf32(self, 8)
    m["oml"] = cf32(self, 8)
    NT = self.NT
    NBK = NT // 128
    m["gatt"] = carve(self, 8 * T).rearrange("p (h t) -> p h t", t=T)
    m["KT"] = carve(self, 2 * NT).rearrange("p (k t) -> p k t", k=2)
    m["Vaug"] = carve(self, NBK * 2 * 129).rearrange("p (b k c) -> p b k c", k=2, c=129)
    m["ikT"] = carve(self, NT)
    m["trineg"] = cf32(self, 128)
    m["tri01"] = carve(self, 128)
    m["anorm"] = cf32(self, 4)
    self.ar_persist = self.ar_off
    self.hf32 = self.arena[:, self.AR - 2 * self.KD * 128:self.AR].bitcast(F32).rearrange("p (c t) -> p c t", t=128)
    self.ar_top = self.AR - 2 * self.KD * 128
    self.identb = P.sb("identb", [128, 128], BF16)


def slab_fm(self, l, c0, n128, tn, consume, fix32=False):
    P, KD, hT, pb = self.P, self.KD, self.hT, self.pb
    win = self.w["w_in_fm"][l]
    hf32 = self.hf32 if fix32 else None
    i = 0
    while i < n128:
        n = 1
        v, k = self.wload(win[self.fm_idx[c0 + i * 128]], KD, 128)
        for j in range(n):
            bi = self.slab_i % 4
            self.slab_i += 1
            bank, bk = pb[bi], "pb%d" % bi

            def mm(e, v=v, bank=bank, j=j):
                for kk in range(KD):
                    ins = e.matmul(bank[:, :tn], lhsT=v[:, kk, j * 128:(j + 1) * 128], rhs=hT[:, kk, :tn],
                                   start=(kk == 0), stop=(kk == KD - 1))
                return ins
            P.op("pe", mm, reads=[k, "hT"], writes=[bk])
            if fix32:
                sl = self.slot_i % self.NSLOT
                self.slot_i += 1
                w32 = self.slots[sl][:, :2 * KD * 128].bitcast(F32).rearrange("p (k c) -> p k c", c=128)
                src = win[self.fm_idx[c0 + (i + j) * 128]]
                k32 = "ws%d" % sl
                P.op("sp", lambda e: e.dma_start(out=w32, in_=src), writes=[k32], dma=k32)

                def mm32(e):
                    for kk in range(KD):
                        ins = e.matmul(bank[:, :128], lhsT=w32[:, kk, :], rhs=hf32[:, kk, :], start=(kk == 0), stop=(kk == KD - 1))
                    return ins
                P.op("pe", mm32, reads=[k32, "hf32"], writes=[bk])
            consume(i + j, bank, bk)
        i += n


def proj_tm(self, l, c0, ncols, tn, consume):
    P, KD, hT, pb = self.P, self.KD, self.hT, self.pb
    for p0 in range(0, ncols, 256):
        pn = min(256, ncols - p0)
        if pn == 256:
            v, k = self.wload(self.w["w_in_tm"][l, self.tm_idx[c0 + p0]], KD, 256)
        else:
            assert pn == 16
            v, k = self.wload(self.w["w_in_iw"][l], KD, 16)
        for bi_, b0 in enumerate(range(0, tn, 128)):
            bn = min(128, tn - b0)
            bi = self.slab_i % 4
            self.slab_i += 1
            bank, bk = pb[bi], "pb%d" % bi

            def mm(e, v=v, bank=bank, b0=b0, bn=bn, pn=pn):
                for kk in range(KD):
                    ins = e.matmul(bank[:bn, :pn], lhsT=hT[:, kk, b0:b0 + bn], rhs=v[:, kk, :pn],
                                   start=(kk == 0), stop=(kk == KD - 1))
                return ins
            P.op("pe", mm, reads=[k, "hT"], writes=[bk])
            consume(bi_, bn, p0, pn, bank, bk)


def rotary_fm(self, src, skey, dst, dkey, tn, tab, half=64, post=None, cs=None, cskey="cs"):
    P, m = self.P, self.m
    cs = m["cs"] if cs is None else cs
    t1, t2 = m["t1"], m["t2"]
    h = half
    lo, hi = slice(0, h), slice(h, 2 * h)
    P.op("dve", lambda e: e.tensor_tensor(out=t1[:2 * h, :tn], in0=src[:2 * h, :tn], in1=cs[:2 * h, 0, :tn], op=ALU.mult),
         reads=[skey, cskey], writes=["t1"])
    P.op("dve", lambda e: e.tensor_tensor(out=t2[lo, :tn], in0=src[hi, :tn], in1=cs[hi, 1, :tn], op=ALU.mult),
         reads=[skey, cskey], writes=["t2a"])
    P.op("dve", lambda e: e.tensor_tensor(out=t2[hi, :tn], in0=src[lo, :tn], in1=cs[lo, 1, :tn], op=ALU.mult),
         reads=[skey, cskey], writes=["t2b"])
    P.op("dve", lambda e: e.tensor_tensor(out=t1[lo, :tn], in0=t1[lo, :tn], in1=t2[lo, :tn], op=ALU.subtract),
         reads=["t1", "t2a"], writes=["t1"])
    P.op("dve", lambda e: e.tensor_tensor(out=t1[hi, :tn], in0=t1[hi, :tn], in1=t2[hi, :tn], op=ALU.add),
         reads=["t1", "t2b"], writes=["t1"])
    if 2 * h < 128:
        P.op("dve", lambda e: e.tensor_copy(out=t1[2 * h:, :tn], in_=src[2 * h:, :tn]), reads=[skey], writes=["t1"])
    post(t1)


def load_cs(self, which, t0, tn, cs=None, cskey="cs"):
    P, m = self.P, self.m
    cs = m["cs"] if cs is None else cs
    for a in range(2):
        src = self.tabs[2 * which + a, :, t0:t0 + tn]
        P.op("sp", lambda e, a=a, src=src: e.dma_start(out=cs[:, a, :tn], in_=src, allow_slow_non_contiguous=(tn < 128)),
             writes=[cskey], dma=cskey)


def rms_fm(self, ob, ok, gain_col, out_t, okey, tn):
    P, pb, ones, rstd = self.P, self.pb, self.ones, self.rstd
    sq = self.sq[0]
    P.op("act", lambda e: e.activation(out=sq[:, :tn], in_=ob[:, :tn], func=AF.Square), reads=[ok], writes=["sq0"])
    nbk, nk = pb[7], "pb7"
    P.op("pe", lambda e: e.matmul(nbk[:, :tn], lhsT=ones[:], rhs=sq[:, :tn], start=True, stop=True),
         reads=["sq0", "ones"], writes=[nk])
    P.op("dve", lambda e: e.tensor_scalar(out=rstd[:, :tn], in0=nbk[:, :tn], scalar1=1.0 / 128, scalar2=EPS,
                                          op0=ALU.mult, op1=ALU.add), reads=[nk], writes=["rstd"])
    P.op("act", lambda e: e.sqrt(out=rstd[:, :tn], in_=rstd[:, :tn]), reads=["rstd"], writes=["rstd"])
    P.op("dve", lambda e: e.reciprocal(out=rstd[:, :tn], in_=rstd[:, :tn]), reads=["rstd"], writes=["rstd"])
    P.op("dve", lambda e: e.scalar_tensor_tensor(out=out_t[:, :tn], in0=ob[:, :tn], scalar=gain_col,
                                                 in1=rstd[:, :tn], op0=ALU.mult, op1=ALU.mult),
         reads=[ok, "nrm", "rstd"], writes=[okey])


def chunk_core(self, h, b, r0, C, cols, S, Sb, Vtm, ktm, qT, AT, ob, ok, decay_fn):
    P, pb = self.P, self.pb
    rs = slice(r0, r0 + C)

    def mo(e):
        e.matmul(ob[:, cols], lhsT=Vtm[rs, b, h * 128:(h + 1) * 128], rhs=AT[rs, rs], start=True, stop=False)
        return e.matmul(ob[:, cols], lhsT=Sb[:, h, :], rhs=qT[:, cols], start=False, stop=True)
    P.op("pe", mo, reads=["Vtm", "AT", "Sb", "qT"], writes=[ok])
    ub, uk = pb[6], "pb6"
    P.op("pe", lambda e: e.matmul(ub[:, :128], lhsT=ktm[rs, b, :], rhs=Vtm[rs, b, h * 128:(h + 1) * 128],
                                  start=True, stop=True), reads=["ktm", "Vtm"], writes=[uk])
    P.op("dve", lambda e: e.tensor_tensor(out=S[:, h, :], in0=ub[:, :128], in1=S[:, h, :], op=ALU.add),
         reads=[uk, "S"], writes=["S"])
    decay_fn()


def mix_init(self, l, ti, t0, tn, is_sample):
    P, m = self.P, self.m
    for nm, st, nr in (("ret", "state_ret", "ret_norm"), ("hg", "state_hgrn", "hg_norm")):
        S, Sb = m["S_" + nm], m["Sb_" + nm]
        if is_sample:
            P.op("sp", lambda e, S=S, st=st: e.dma_start(out=S, in_=self.w[st][l].rearrange("h d v -> d h v")),
                 writes=["S"], dma="S" + nm)
        else:
            P.op("dve", lambda e, S=S: e.memset(S, 0.0), writes=["S"])
        P.op("act", lambda e, S=S, Sb=Sb: e.copy(out=Sb, in_=S), reads=["S"], writes=["Sb"])
        g = m["nrm_" + nm]
        P.op("sp", lambda e, g=g, nr=nr: e.dma_start(out=g, in_=self.w[nr][l].rearrange("(h p) -> p h", p=128),
                                                     allow_slow_non_contiguous=True), writes=["nrm"], dma="nrm" + nm)
    causT, caus64 = m["causT"], m["caus64"]
    P.op("pool", lambda e: e.dma_start(out=causT, in_=self.causT_d), writes=["causT"], dma="causT")
    P.op("pool", lambda e: e.dma_start(out=caus64, in_=self.caus64_d), writes=["caus64"], dma="caus64")
    identb, ident = self.identb, self.ident
    P.op("act", lambda e: e.copy(out=identb[:], in_=ident[:]), reads=["ident"], writes=["identb"])
    lb, oml = m["lb"], m["oml"]
    if l == 0:
        P.op("dve", lambda e: e.memset(lb, 0.0), writes=["lb"])
    else:
        raw = self.rstd
        P.op("sp", lambda e: e.dma_start(out=raw[:, :16].rearrange("p (l h) -> p l h", l=2),
                                         in_=self.w["hg_lb_raw"].rearrange("l (h p) -> p l h", p=128),
                                         allow_slow_non_contiguous=True), writes=["rstd"], dma="lbraw")
        P.op("dve", lambda e: e.tensor_tensor(out=lb, in0=raw[:, 8:16], in1=raw[:, 0:8], op=ALU.subtract),
             reads=["rstd"], writes=["lb"])
        P.op("act", lambda e: e.activation(out=lb, in_=lb, func=AF.Sigmoid), reads=["lb"], writes=["lb"])
    P.op("dve", lambda e: e.tensor_scalar(out=oml, in0=lb, scalar1=-1.0, scalar2=1.0, op0=ALU.mult, op1=ALU.add),
         reads=["lb"], writes=["oml"])


def k_transposes(self, kT, ktm, nb, C128):
    P, identb = self.P, self.identb
    tb = self.pbT
    for b in range(nb):
        P.op("pe", lambda e, b=b: e.transpose(tb[:C128, :128], kT[:, b * C128:(b + 1) * C128], identb[:, :]),
             reads=["kT", "identb"], writes=["pb7"])
        P.op("act", lambda e, b=b: e.copy(out=ktm[:C128, b, :], in_=tb[:C128, :128]), reads=["pb7"], writes=["ktm"])


def mix_retention(self, l, ti, t0, tn, is_sample):
    P, m, pb, sec = self.P, self.m, self.pb, self.sec
    self.ar_off = self.ar_persist
    C = min(128, tn)
    nb = max(1, tn // 128)
    dec = cf32(self, 2048).rearrange("p (a h t) -> p a h t", a=2, h=8)
    m["cs"] = cf32(self, 2 * tn).rearrange("p (a t) -> p a t", a=2)
    Vtm = carve(self, nb * 1024).rearrange("p (b c) -> p b c", c=1024)
    qT, kT = carve(self, tn), carve(self, tn)
    ktm = carve(self, nb * 128).rearrange("p (b c) -> p b c", c=128)
    AT = carve(self, 128)
    m["t1"], m["t2"], t3 = cf32(self, tn), cf32(self, tn), cf32(self, tn)
    S, Sb, gret, causT, rng = m["S_ret"], m["Sb_ret"], m["gret"], m["causT"], m["nrm_ret"]
    fix = (ti == 0) and (not is_sample) and tn >= 128
    if fix:
        qT32, kT32 = cf32(self, 128), cf32(self, 128)
        assert self.ar_off <= self.ar_top
    P.op("sp", lambda e: e.dma_start(out=dec.rearrange("p a h t -> p (a h t)"), in_=self.dec_d), writes=["dec"], dma="dec")
    load_cs(self, 0, t0, tn)

    def cons_v(bi_, bn, p0, pn, bank, bk):
        P.op("act", lambda e: e.copy(out=Vtm[:bn, bi_, p0:p0 + pn], in_=bank[:bn, :pn]), reads=[bk], writes=["Vtm"])
    proj_tm(self, l, sec["r_v"][0], 1024, tn, cons_v)
    for h in range(8):
        def cons_q(i, bank, bk, h=h):
            def post(t1):
                P.op("dve", lambda e: e.tensor_tensor(
                    out=qT[:, :tn].rearrange("p (b t) -> p b t", t=C), in0=t1[:, :tn].rearrange("p (b t) -> p b t", t=C),
                    in1=dec[:, 0, h:h + 1, :C].to_broadcast([128, nb, C]), op=ALU.mult),
                    reads=["t1", "dec"], writes=["qT"])
                if fix:
                    P.op("dve", lambda e: e.tensor_tensor(out=qT32[:, :128], in0=t1[:, :128], in1=dec[:, 0, h, :128], op=ALU.mult),
                         reads=["t1", "dec"], writes=["qT32"])
            rotary_fm(self, bank, bk, None, None, tn, 0, post=post)
        slab_fm(self, l, sec["r_q"][0] + h * 128, 1, tn, cons_q, fix32=fix)

        def cons_k(i, bank, bk, h=h):
            def post(t1):
                P.op("dve", lambda e: e.tensor_tensor(
                    out=kT[:, :tn].rearrange("p (b t) -> p b t", t=C), in0=t1[:, :tn].rearrange("p (b t) -> p b t", t=C),
                    in1=dec[:, 1, h:h + 1, :C].to_broadcast([128, nb, C]), op=ALU.mult),
                    reads=["t1", "dec"], writes=["kT"])
                if fix:
                    P.op("dve", lambda e: e.tensor_tensor(out=kT32[:, :128], in0=t1[:, :128], in1=dec[:, 1, h, :128], op=ALU.mult),
                         reads=["t1", "dec"], writes=["kT32"])
            rotary_fm(self, bank, bk, None, None, tn, 0, post=post)
        slab_fm(self, l, sec["r_k"][0] + h * 128, 1, tn, cons_k, fix32=fix)
        k_transposes(self, kT, ktm, nb, C)
        ob, ok = pb[4], "pb4"
        gC = float(self.gamC[h]) if C == 128 else float(self.gam1[h])
        for b in range(nb):
            ab, ak = pb[5], "pb5"
            if fix and b == 0:
                P.op("pe", lambda e: e.matmul(ab[:C, :C], lhsT=kT32[:, :128], rhs=qT32[:, :128], start=True, stop=True),
                     reads=["kT32", "qT32"], writes=[ak])
            else:
                P.op("pe", lambda e, b=b: e.matmul(ab[:C, :C], lhsT=kT[:, b * C:(b + 1) * C], rhs=qT[:, b * C:(b + 1) * C],
                                                   start=True, stop=True), reads=["kT", "qT"], writes=[ak])
            P.op("dve", lambda e: e.tensor_tensor(out=AT[:C, :C], in0=ab[:C, :C], in1=causT[:C, :C], op=ALU.mult),
                 reads=[ak, "causT"], writes=["AT"])

            def decay(h=h, gC=gC):
                P.op("act", lambda e: e.mul(out=S[:, h, :], in_=S[:, h, :], mul=gC), reads=["S"], writes=["S"])
                P.op("act", lambda e: e.copy(out=Sb[:, h, :], in_=S[:, h, :]), reads=["S"], writes=["Sb"])
            chunk_core(self, h, b, 0, C, slice(b * C, (b + 1) * C), S, Sb, Vtm, ktm, qT, AT, ob, ok, decay)
        rms_fm(self, ob, ok, rng[:, h:h + 1], t3, "t3", tn)

        def cons_g(i, bank, bk, h=h):
            sq1 = self.sq[1]
            P.op("act", lambda e: e.activation(out=sq1[:, :tn], in_=bank[:, :tn], func=AF.Silu), reads=[bk], writes=["sq1"])
            P.op("dve", lambda e: e.tensor_tensor(out=gret[:, h, :tn], in0=t3[:, :tn], in1=sq1[:, :tn], op=ALU.mult),
                 reads=["t3", "sq1"], writes=["gret"])
        slab_fm(self, l, sec["r_g"][0] + h * 128, 1, tn, cons_g)
    last_prompt = (not is_sample) and (t0 + tn >= self.NT)
    if last_prompt or is_sample:
        dst = (self.rss if is_sample else self.rsp)[l].rearrange("h d v -> d h v")
        P.op("sp", lambda e: e.dma_start(out=dst, in_=S), reads=["S"], dma="Sout")


def mix_hgrn(self, l, ti, t0, tn, is_sample):
    P, m, pb, sec = self.P, self.m, self.pb, self.sec
    self.ar_off = self.ar_persist
    C = 64 if tn >= 64 else tn
    nch = tn // C
    nb = max(1, tn // 128)
    C128 = min(128, tn)
    cpb = C128 // C
    Vtm = carve(self, nb * 1024).rearrange("p (b c) -> p b c", c=1024)
    qT, kT = carve(self, tn), carve(self, tn)
    ktm = carve(self, nb * 128).rearrange("p (b c) -> p b c", c=128)
    AT = carve(self, 128)
    t1, t2, t3, t4, t5, t6 = [cf32(self, tn) for _ in range(6)]
    em, el = cf32(self, max(nch, 1)), cf32(self, max(nch, 1))
    S, Sb, ghg, gn = m["S_hg"], m["Sb_hg"], m["ghg"], m["nrm_hg"]
    lb, oml = m["lb"], m["oml"]
    mask = m["caus64"] if C == 64 else m["causT"]
    fix = (ti == 0) and (not is_sample) and tn >= 128
    if fix:
        qT32, kT32 = cf32(self, 128), cf32(self, 128)
        assert self.ar_off <= self.ar_top

    def cons_v(bi_, bn, p0, pn, bank, bk):
        P.op("act", lambda e: e.copy(out=Vtm[:bn, bi_, p0:p0 + pn], in_=bank[:bn, :pn]), reads=[bk], writes=["Vtm"])
    proj_tm(self, l, sec["h_i"][0], 1024, tn, cons_v)
    v3 = lambda t: t[:, :tn].rearrange("p (n c) -> p n c", c=C)
    mid = C // 2 - 1
    for h in range(8):
        def cons_f(i, bank, bk, h=h):
            P.op("act", lambda e: e.activation(out=t1[:, :tn], in_=bank[:, :tn], func=AF.Exp, scale=-1.0), reads=[bk], writes=["t1"])
            P.op("act", lambda e: e.activation(out=t4[:, :tn], in_=bank[:, :tn], func=AF.Sigmoid, scale=-1.0), reads=[bk], writes=["t4"])
            P.op("dve", lambda e: e.tensor_scalar(out=t4[:, :tn], in0=t4[:, :tn], scalar1=oml[:, h:h + 1], scalar2=None, op0=ALU.mult),
                 reads=["t4", "oml"], writes=["t4"])
            P.op("act", lambda e: e.activation(out=t2[:, :tn], in_=t1[:, :tn], func=AF.Ln, bias=1.0), reads=["t1"], writes=["t2"])
            if l == 0:
                P.op("dve", lambda e: e.tensor_scalar(out=t3[:, :tn], in0=t2[:, :tn], scalar1=-1.0, scalar2=None, op0=ALU.mult),
                     reads=["t2"], writes=["t3"])
            else:
                P.op("dve", lambda e: e.tensor_scalar(out=t3[:, :tn], in0=t1[:, :tn], scalar1=lb[:, h:h + 1], scalar2=None, op0=ALU.mult),
                     reads=["t1", "lb"], writes=["t3"])
                P.op("act", lambda e: e.activation(out=t3[:, :tn], in_=t3[:, :tn], func=AF.Ln, bias=1.0), reads=["t3"], writes=["t3"])
                P.op("dve", lambda e: e.tensor_tensor(out=t3[:, :tn], in0=t3[:, :tn], in1=t2[:, :tn], op=ALU.subtract),
                     reads=["t3", "t2"], writes=["t3"])
        slab_fm(self, l, sec["h_f"][0] + h * 128, 1, tn, cons_f, fix32=fix)
        src, dst, ks, kd = t3, t5, "t3", "t5"
        sft = 1
        while sft < C:
            P.op("act", lambda e, src=src, dst=dst: e.copy(out=dst[:, :tn], in_=src[:, :tn]), reads=[ks], writes=[kd])
            P.op("dve", lambda e, src=src, dst=dst, sft=sft: e.tensor_tensor(
                out=v3(dst)[:, :, sft:], in0=v3(src)[:, :, sft:], in1=v3(src)[:, :, :C - sft], op=ALU.add),
                reads=[ks], writes=[kd])
            src, dst, ks, kd = dst, src, kd, ks
            sft *= 2
        bt, kb_ = src, ks
        if C > 1:
            P.op("dve", lambda e, bt=bt: e.tensor_tensor(out=v3(t6), in0=v3(bt), in1=v3(bt)[:, :, mid:mid + 1].to_broadcast([128, nch, C]),
                                                       op=ALU.subtract), reads=[kb_], writes=["t6"])
            P.op("act", lambda e, bt=bt: e.activation(out=em[:, :nch], in_=v3(bt)[:, :, mid], func=AF.Exp), reads=[kb_], writes=["em"])
            P.op("dve", lambda e, bt=bt: e.tensor_tensor(out=el[:, :nch], in0=v3(bt)[:, :, C - 1], in1=v3(bt)[:, :, mid], op=ALU.subtract),
                 reads=[kb_], writes=["el"])
            P.op("act", lambda e: e.activation(out=el[:, :nch], in_=el[:, :nch], func=AF.Exp), reads=["el"], writes=["el"])
            dd, kdd = t6, "t6"
        else:
            P.op("dve", lambda e: e.memset(em[:, :1], 1.0), writes=["em"])
            P.op("act", lambda e, bt=bt: e.activation(out=el[:, :1], in_=bt[:, :1], func=AF.Exp), reads=[kb_], writes=["el"])
            dd, kdd = bt, kb_
        P.op("act", lambda e, dd=dd: e.activation(out=t2[:, :tn], in_=dd[:, :tn], func=AF.Exp), reads=[kdd], writes=["t2"])
        P.op("act", lambda e, dd=dd: e.activation(out=t1[:, :tn], in_=dd[:, :tn], func=AF.Exp, scale=-1.0), reads=[kdd], writes=["t1"])
        P.op("dve", lambda e: e.tensor_tensor(out=kT[:, :tn], in0=t4[:, :tn], in1=t1[:, :tn], op=ALU.mult),
             reads=["t4", "t1"], writes=["kT"])
        if fix:
            P.op("dve", lambda e: e.tensor_tensor(out=kT32[:, :128], in0=t4[:, :128], in1=t1[:, :128], op=ALU.mult),
                 reads=["t4", "t1"], writes=["kT32"])

        def cons_q(i, bank, bk, h=h):
            sq1 = self.sq[1]
            P.op("act", lambda e: e.activation(out=sq1[:, :tn], in_=bank[:, :tn], func=AF.Silu), reads=[bk], writes=["sq1"])
            P.op("dve", lambda e: e.tensor_tensor(out=qT[:, :tn], in0=sq1[:, :tn], in1=t2[:, :tn], op=ALU.mult),
                 reads=["sq1", "t2"], writes=["qT"])
            if fix:
                P.op("dve", lambda e: e.tensor_tensor(out=qT32[:, :128], in0=sq1[:, :128], in1=t2[:, :128], op=ALU.mult),
                     reads=["sq1", "t2"], writes=["qT32"])
        slab_fm(self, l, sec["h_q"][0] + h * 128, 1, tn, cons_q, fix32=fix)
        k_transposes(self, kT, ktm, nb, C128)
        ob, ok = pb[4], "pb4"
        for b in range(nb):
            ab, ak = pb[5], "pb5"
            if fix and b == 0:
                P.op("pe", lambda e: e.matmul(ab[:C128, :C128], lhsT=kT32[:, :128], rhs=qT32[:, :128], start=True, stop=True),
                     reads=["kT32", "qT32"], writes=[ak])
            else:
                P.op("pe", lambda e, b=b: e.matmul(ab[:C128, :C128], lhsT=kT[:, b * C128:(b + 1) * C128],
                                                   rhs=qT[:, b * C128:(b + 1) * C128], start=True, stop=True),
                     reads=["kT", "qT"], writes=[ak])
            P.op("dve", lambda e: e.tensor_tensor(out=AT[:C128, :C128], in0=ab[:C128, :C128], in1=mask[:C128, :C128], op=ALU.mult),
                 reads=[ak, "caus64", "causT"], writes=["AT"])
            for c in range(cpb):
                ch = b * cpb + c
                P.op("dve", lambda e, h=h, ch=ch: e.tensor_scalar(out=S[:, h, :], in0=S[:, h, :], scalar1=em[:, ch:ch + 1],
                                                                 scalar2=None, op0=ALU.mult), reads=["S", "em"], writes=["S"])
                P.op("act", lambda e, h=h: e.copy(out=Sb[:, h, :], in_=S[:, h, :]), reads=["S"], writes=["Sb"])

                def decay(h=h, ch=ch):
                    P.op("dve", lambda e: e.tensor_scalar(out=S[:, h, :], in0=S[:, h, :], scalar1=el[:, ch:ch + 1],
                                                          scalar2=None, op0=ALU.mult), reads=["S", "el"], writes=["S"])
                chunk_core(self, h, b, c * C, C, slice(b * C128 + c * C, b * C128 + (c + 1) * C), S, Sb, Vtm, ktm, qT, AT, ob, ok, decay)
        rms_fm(self, ob, ok, gn[:, h:h + 1], t3, "t3", tn)

        def cons_g(i, bank, bk, h=h):
            sq1 = self.sq[1]
            P.op("act", lambda e: e.activation(out=sq1[:, :tn], in_=bank[:, :tn], func=AF.Sigmoid), reads=[bk], writes=["sq1"])
            P.op("dve", lambda e: e.tensor_tensor(out=ghg[:, h, :tn], in0=t3[:, :tn], in1=sq1[:, :tn], op=ALU.mult),
                 reads=["t3", "sq1"], writes=["ghg"])
        slab_fm(self, l, sec["h_g"][0] + h * 128, 1, tn, cons_g)
    last_prompt = (not is_sample) and (t0 + tn >= self.NT)
    if last_prompt or is_sample:
        dst = (self.hss if is_sample else self.hsp)[l].rearrange("h d v -> d h v")
        P.op("sp", lambda e: e.dma_start(out=dst, in_=S), reads=["S"], dma="Sout2")


def mix_merge(self, l, ti, t0, tn, branches):
    P, m, pb, KD, sec = self.P, self.m, self.pb, self.KD, self.sec
    self.ar_off = self.ar_persist
    merged = carve(self, KD * tn).rearrange("p (c t) -> p c t", t=tn)
    acc = cf32(self, tn)
    nbr = len(branches)
    for fo in range(KD):
        for bi, (wn, gk, gsec) in enumerate(branches):
            gbuf = m[gk]
            v, k = self.wload(self.w[wn][l, fo], 8, 128)
            ubk, uk = pb[4 + bi % 2], "pb%d" % (4 + bi % 2)

            def mmu(e, v=v, ubk=ubk, gbuf=gbuf):
                for kk in range(8):
                    ins = e.matmul(ubk[:, :tn], lhsT=v[:, kk, :], rhs=gbuf[:, kk, :tn], start=(kk == 0), stop=(kk == 7))
                return ins
            P.op("pe", mmu, reads=[k, gk], writes=[uk])

            def cons_gate(i, bank, bk, fo=fo, ubk=ubk, uk=uk, bi=bi):
                sq1, sk = self.sq[bi % 2], "sq%d" % (bi % 2)
                P.op("act", lambda e: e.activation(out=sq1[:, :tn], in_=bank[:, :tn], func=AF.Sigmoid), reads=[bk], writes=[sk])
                last = bi == nbr - 1
                dst, dk = (merged[:, fo, :tn], "merged") if last else (acc[:, :tn], "acc")
                if bi == 0:
                    P.op("dve", lambda e: e.tensor_tensor(out=dst, in0=sq1[:, :tn], in1=ubk[:, :tn], op=ALU.mult),
                         reads=[sk, uk], writes=[dk])
                else:
                    P.op("dve", lambda e: e.tensor_tensor(out=sq1[:, :tn], in0=sq1[:, :tn], in1=ubk[:, :tn], op=ALU.mult),
                         reads=[sk, uk], writes=[sk])
                    P.op("dve", lambda e: e.tensor_tensor(out=dst, in0=sq1[:, :tn], in1=acc[:, :tn], op=ALU.add),
                         reads=[sk, "acc"], writes=[dk])
            slab_fm(self, l, sec[gsec][0] + fo * 128, 1, tn, cons_gate)
    wout = self.w["w_out"][l]
    slow = tn < 128
    for fo0 in range(0, KD, 2):
        fn = min(2, KD - fo0)
        assert fn == 2
        v, k = self.wload(wout[fo0 // 2], KD, 256)
        for j in range(fn):
            fo = fo0 + j
            by, ky = pb[4 + fo % 2], "pb%d" % (4 + fo % 2)

            def mmo(e, v=v, j=j, by=by):
                for kk in range(KD):
                    ins = e.matmul(by[:, :tn], lhsT=v[:, kk, j * 128:(j + 1) * 128], rhs=merged[:, kk, :tn],
                                   start=(kk == 0), stop=(kk == KD - 1))
                return ins
            P.op("pe", mmo, reads=[k, "merged"], writes=[ky])
            xc = self.xc[fo % 2]
            kx = "xc%d" % (fo % 2)
            src = self.xT[:, fo, t0:t0 + tn]
            P.op("sp", lambda e, xc=xc, src=src: e.dma_start(out=xc[:, :tn], in_=src, allow_slow_non_contiguous=slow),
                 reads=["xT%d" % fo], writes=[kx], dma=kx)
            P.op("dve", lambda e, xc=xc, by=by: e.tensor_tensor(out=xc[:, :tn], in0=by[:, :tn], in1=xc[:, :tn], op=ALU.add),
                 reads=[ky, kx], writes=[kx])
            P.op("sp", lambda e, xc=xc, src=src: e.dma_start(out=src, in_=xc[:, :tn], allow_slow_non_contiguous=slow),
                 reads=[kx], writes=["xT%d" % fo], dma=kx + "s")


def tm_out(self, src_fm, skey, tn, dst_fn):
    P, ident, pb = self.P, self.ident, self.pb
    xio = self.xio
    for b0 in range(0, tn, 128):
        bn = min(128, tn - b0)
        P.op("pe", lambda e, b0=b0, bn=bn: e.transpose(pb[6][:bn, :128], src_fm[:, b0:b0 + bn], ident[:, :]),
             reads=[skey, "ident"], writes=["pb6"])
        P.op("act", lambda e, bn=bn: e.copy(out=xio[:bn, :128], in_=pb[6][:bn, :128]), reads=["pb6"], writes=["xio"])
        dst = dst_fn(b0, bn)
        P.op("sp", lambda e, dst=dst, bn=bn: e.dma_start(out=dst, in_=xio[:bn, :128]), reads=["xio"], dma="tmout")


def mix_dsa_proj(self, l, ti, t0, tn, is_sample):
    P, m, pb, sec = self.P, self.m, self.pb, self.sec
    self.ar_off = self.ar_persist
    nb = max(1, tn // 128)
    qTa = carve(self, 8 * tn).rearrange("p (h t) -> p h t", t=tn)
    iqT = carve(self, 16 * tn).rearrange("p (h t) -> p h t", t=tn)
    wi = cf32(self, nb * 16).rearrange("p (b c) -> p b c", c=16)
    m["qTa"], m["iqT"], m["wi"] = qTa, iqT, wi
    an = m["anorm"]
    KT, Vaug, ikT = m["KT"], m["Vaug"], m["ikT"]
    if is_sample:
        KT = carve(self, 16).rearrange("p (k t) -> p k t", k=2)[:, :, 0:1]
        ikT = carve(self, 16)[:, 0:1]
        m["KTs"], m["ikTs"] = KT, ikT
        m["vself"] = cf32(self, 256)
    self.ar_attn = self.ar_off
    csA = cf32(self, 2 * tn).rearrange("p (a t) -> p a t", a=2)
    csI = cf32(self, 2 * tn).rearrange("p (a t) -> p a t", a=2)
    m["t1"], m["t2"], t3 = cf32(self, tn), cf32(self, tn), cf32(self, tn)
    if ti == 0 or is_sample:
        for j, nm in enumerate(("q_norm", "k_norm", "idx_k_g", "idx_k_b")):
            P.op("sp", lambda e, j=j, nm=nm: e.dma_start(out=an[:, j:j + 1], in_=self.w[nm][l].rearrange("(p o) -> p o", o=1),
                                                         allow_slow_non_contiguous=True), writes=["nrm"], dma="anorm")
        P.op("sp", lambda e: e.dma_start(out=m["trineg"], in_=self.trineg_d), writes=["trineg"], dma="trineg")
        P.op("pool", lambda e: e.dma_start(out=m["tri01"], in_=self.tri01_d), writes=["tri01"], dma="tri01")
        if not is_sample:
            P.op("dve", lambda e: e.memset(m["Vaug"][:, :, :, 128:129], 1.0), writes=["Vaug1"])
    load_cs(self, 1, t0, tn, cs=csA, cskey="csA")
    load_cs(self, 2, t0, tn, cs=csI, cskey="csI")
    t0k = 0 if is_sample else t0
    for h in range(8):
        def cons_q(i, bank, bk, h=h):
            rms_fm(self, bank, bk, an[:, 0:1], t3, "t3", tn)

            def post(t1):
                P.op("act", lambda e: e.mul(out=qTa[:, h, :tn], in_=t1[:, :tn], mul=128 ** -0.5), reads=["t1"], writes=["qTa"])
            rotary_fm(self, t3, "t3", None, None, tn, 0, post=post, cs=csA, cskey="csA")
        slab_fm(self, l, sec["a_q"][0] + h * 128, 1, tn, cons_q)
    for kv in range(2):
        def cons_k(i, bank, bk, kv=kv):
            rms_fm(self, bank, bk, an[:, 1:2], t3, "t3", tn)

            def post(t1):
                P.op("act", lambda e: e.copy(out=KT[:, kv, t0k:t0k + tn], in_=t1[:, :tn]), reads=["t1"], writes=["KT"])
                if is_sample:
                    P.op("sp", lambda e: e.dma_start(out=self.ks[l, 0, kv, :].rearrange("(p o) -> p o", o=1), in_=t1[:, 0:1],
                                                     allow_slow_non_contiguous=True), reads=["t1"], dma="tmout")
                else:
                    tm_out(self, t1, "t1", tn, lambda b0, bn: self.kp[l, t0 + b0:t0 + b0 + bn, kv, :])
            rotary_fm(self, t3, "t3", None, None, tn, 0, post=post, cs=csA, cskey="csA")
        slab_fm(self, l, sec["a_k"][0] + kv * 128, 1, tn, cons_k)
    xio = self.xio

    def cons_v(bi_, bn, p0, pn, bank, bk):
        P.op("act", lambda e: e.copy(out=xio[:bn, :256], in_=bank[:bn, :256]), reads=[bk], writes=["xio"])
        if is_sample:
            P.op("sp", lambda e: e.dma_start(out=self.vs[l, 0].rearrange("k d -> (k d)").rearrange("(o n) -> o n", o=1),
                                             in_=xio[:1, :256]), reads=["xio"], dma="tmout")
            vself = m["vself"]
            P.op("dve", lambda e: e.tensor_copy(out=vself[:1, :256], in_=xio[:1, :256]), reads=["xio"], writes=["vself"])
        else:
            gb = (t0 + bi_ * 128) // 128
            P.op("sp", lambda e: e.dma_start(out=self.vp[l, t0 + bi_ * 128:t0 + bi_ * 128 + bn].rearrange("t k d -> t (k d)"),
                                             in_=xio[:bn, :256]), reads=["xio"], dma="tmout")
            P.op("dve", lambda e: e.tensor_copy(out=Vaug[:bn, gb, :, 0:128], in_=xio[:bn, :256].rearrange("p (k d) -> p k d", k=2)),
                 reads=["xio"], writes=["Vaug"])
    proj_tm(self, l, sec["a_v"][0], 256, tn, cons_v)
    for ih in range(16):
        def cons_iq(i, bank, bk, ih=ih):
            def post(t1):
                P.op("act", lambda e: e.mul(out=iqT[:, ih, :tn], in_=t1[:, :tn], mul=128 ** -0.5), reads=["t1"], writes=["iqT"])
            rotary_fm(self, bank, bk, None, None, tn, 0, half=32, post=post, cs=csI, cskey="csI")
        slab_fm(self, l, sec["i_q"][0] + ih * 128, 1, tn, cons_iq)
    ones, rstd = self.ones, self.rstd

    def cons_ik(i, bank, bk):
        sq = self.sq[0]
        P.op("act", lambda e: e.copy(out=t3[:, :tn], in_=bank[:, :tn]), reads=[bk], writes=["t3"])
        nbk, nk = pb[7], "pb7"
        P.op("pe", lambda e: e.matmul(nbk[:, :tn], lhsT=ones[:], rhs=t3[:, :tn], start=True, stop=True),
             reads=["t3", "ones"], writes=[nk])
        P.op("dve", lambda e: e.scalar_tensor_tensor(out=t3[:, :tn], in0=nbk[:, :tn], scalar=-1.0 / 128, in1=t3[:, :tn],
                                                     op0=ALU.mult, op1=ALU.add), reads=[nk, "t3"], writes=["t3"])
        P.op("act", lambda e: e.activation(out=sq[:, :tn], in_=t3[:, :tn], func=AF.Square), reads=["t3"], writes=["sq0"])
        P.op("pe", lambda e: e.matmul(nbk[:, :tn], lhsT=ones[:], rhs=sq[:, :tn], start=True, stop=True),
             reads=["sq0", "ones"], writes=[nk])
        P.op("dve", lambda e: e.tensor_scalar(out=rstd[:, :tn], in0=nbk[:, :tn], scalar1=1.0 / 128, scalar2=EPS,
                                              op0=ALU.mult, op1=ALU.add), reads=[nk], writes=["rstd"])
        P.op("act", lambda e: e.sqrt(out=rstd[:, :tn], in_=rstd[:, :tn]), reads=["rstd"], writes=["rstd"])
        P.op("dve", lambda e: e.reciprocal(out=rstd[:, :tn], in_=rstd[:, :tn]), reads=["rstd"], writes=["rstd"])
        P.op("dve", lambda e: e.scalar_tensor_tensor(out=t3[:, :tn], in0=t3[:, :tn], scalar=an[:, 2:3], in1=rstd[:, :tn],
                                                     op0=ALU.mult, op1=ALU.mult), reads=["t3", "nrm", "rstd"], writes=["t3"])
        P.op("dve", lambda e: e.tensor_scalar(out=t3[:, :tn], in0=t3[:, :tn], scalar1=an[:, 3:4], scalar2=None, op0=ALU.add),
             reads=["t3", "nrm"], writes=["t3"])

        def post(t1):
            P.op("act", lambda e: e.copy(out=ikT[:, t0k:t0k + tn], in_=t1[:, :tn]), reads=["t1"], writes=["ikT"])
            if is_sample:
                P.op("sp", lambda e: e.dma_start(out=self.iks[l, 0, :].rearrange("(p o) -> p o", o=1), in_=t1[:, 0:1],
                                                 allow_slow_non_contiguous=True), reads=["t1"], dma="tmout")
            else:
                tm_out(self, t1, "t1", tn, lambda b0, bn: self.ikp[l, t0 + b0:t0 + b0 + bn, :])
        rotary_fm(self, t3, "t3", None, None, tn, 0, half=32, post=post, cs=csI, cskey="csI")
    slab_fm(self, l, sec["i_k"][0], 1, tn, cons_ik)

    def cons_w(bi_, bn, p0, pn, bank, bk):
        P.op("act", lambda e: e.mul(out=wi[:bn, bi_, :], in_=bank[:bn, :16], mul=16 ** -0.5), reads=[bk], writes=["wi"])
    proj_tm(self, l, sec["i_w"][0], 16, tn, cons_w)


def mix_dsa_attn_prompt(self, l, ti, t0, tn):
    P, m, pb, sec = self.P, self.m, self.pb, self.sec
    self.ar_off = self.ar_attn
    NT = self.NT
    NB = NT // 128
    nsel = self.NSEL_P
    hT_elems = self.KD * self.T
    if 2 * NT <= hT_elems:
        S = self.hT[:, :, :].rearrange("p c t -> p (c t)")[:, :2 * NT].bitcast(F32)
        self.hT_dirty = True
    else:
        S = cf32(self, NT)
    maskf = carve(self, NT)
    maskT = carve(self, NB * 128).rearrange("p (b t) -> p b t", t=128)
    Eb = [carve(self, 512) for _ in range(2)]
    ao = cf32(self, 1024)
    mx = cf32(self, 8)
    rinv = cf32(self, 1)
    qTa, iqT, wi = m["qTa"], m["iqT"], m["wi"]
    KT, Vaug, ikT, gatt = m["KT"], m["Vaug"], m["ikT"], m["gatt"]
    trineg, tri01, identb, ident = m["trineg"], m["tri01"], self.identb, self.ident
    for b in range(tn // 128):
        b0 = b * 128
        gi = (t0 + b0) // 128
        W = (gi + 1) * 128
        dg = slice(gi * 128, W)
        if W > nsel:
            for kb0 in range(0, W, 512):
                kw = min(512, W - kb0)
                for ih in range(16):
                    bank, bk = pb[4 + ih % 2], "pb%d" % (4 + ih % 2)
                    P.op("pe", lambda e, ih=ih, bank=bank, kb0=kb0, kw=kw: e.matmul(
                        bank[:, :kw], lhsT=iqT[:, ih, b0:b0 + 128], rhs=ikT[:, kb0:kb0 + kw], start=True, stop=True),
                        reads=["iqT", "ikT"], writes=[bk])
                    sq, sk = self.sq[ih % 2], "sq%d" % (ih % 2)
                    P.op("act", lambda e, sq=sq, bank=bank, kw=kw: e.activation(out=sq[:, :kw], in_=bank[:, :kw], func=AF.Relu),
                         reads=[bk], writes=[sk])
                    if ih == 0:
                        P.op("dve", lambda e, sq=sq, kb0=kb0, kw=kw: e.tensor_scalar(
                            out=S[:, kb0:kb0 + kw], in0=sq[:, :kw], scalar1=wi[:, b, 0:1], scalar2=None, op0=ALU.mult),
                            reads=[sk, "wi"], writes=["S_sc"])
                    else:
                        P.op("dve", lambda e, sq=sq, kb0=kb0, kw=kw, ih=ih: e.scalar_tensor_tensor(
                            out=S[:, kb0:kb0 + kw], in0=sq[:, :kw], scalar=wi[:, b, ih:ih + 1], in1=S[:, kb0:kb0 + kw],
                            op0=ALU.mult, op1=ALU.add), reads=[sk, "wi", "S_sc"], writes=["S_sc"])
            P.op("dve", lambda e: e.tensor_tensor(out=S[:, dg], in0=S[:, dg], in1=trineg, op=ALU.add),
                 reads=["S_sc", "trineg"], writes=["S_sc"])
            for r in range(nsel // 8):
                P.op("dve", lambda e: e.max(out=mx, in_=S[:, :W]), reads=["S_sc"], writes=["mx"])
                P.op("dve", lambda e: e.match_replace(out=S[:, :W], in_to_replace=mx, in_values=S[:, :W], imm_value=-1e30),
                     reads=["S_sc", "mx"], writes=["S_sc"])
            P.op("dve", lambda e: e.tensor_scalar(out=maskf[:, :W], in0=S[:, :W], scalar1=-1e29, scalar2=None, op0=ALU.is_lt),
                 reads=["S_sc"], writes=["maskf"])
            P.op("dve", lambda e: e.tensor_tensor(out=maskf[:, dg], in0=maskf[:, dg], in1=tri01, op=ALU.mult),
                 reads=["maskf", "tri01"], writes=["maskf"])
        else:
            if gi > 0:
                P.op("dve", lambda e: e.memset(maskf[:, :gi * 128], 1.0), writes=["maskf"])
            P.op("dve", lambda e: e.tensor_copy(out=maskf[:, dg], in_=tri01), reads=["tri01"], writes=["maskf"])
        tb = self.pbT
        for kb in range(gi + 1):
            P.op("pe", lambda e, kb=kb: e.transpose(tb[:, :128], maskf[:, kb * 128:(kb + 1) * 128], identb[:, :]),
                 reads=["maskf", "identb"], writes=["pb7"])
            P.op("act", lambda e, kb=kb: e.copy(out=maskT[:, kb, :], in_=tb[:, :128]), reads=["pb7"], writes=["maskT"])
        for kv in range(2):
            for kb in range(gi + 1):
                lb_, lk = pb[4 + kb % 2], "pb%d" % (4 + kb % 2)
                P.op("pe", lambda e, kb=kb, lb_=lb_: e.matmul(
                    lb_[:, :512].rearrange("p (h t) -> p h t", h=4), lhsT=KT[:, kv, kb * 128:(kb + 1) * 128],
                    rhs=qTa[:, kv * 4:(kv + 1) * 4, b0:b0 + 128], start=True, stop=True), reads=["KT", "qTa"], writes=[lk])
                E, ek = Eb[kb % 2], "E%d" % (kb % 2)
                P.op("act", lambda e, E=E, lb_=lb_: e.activation(out=E[:, :512], in_=lb_[:, :512], func=AF.Exp), reads=[lk], writes=[ek])
                P.op("dve", lambda e, E=E, kb=kb: e.tensor_tensor(
                    out=E[:, :512].rearrange("p (h t) -> p h t", h=4), in0=E[:, :512].rearrange("p (h t) -> p h t", h=4),
                    in1=maskT[:, kb:kb + 1, :].to_broadcast([128, 4, 128]), op=ALU.mult), reads=[ek, "maskT"], writes=[ek])

                if self.cfg.get("debug") and l == 0 and ti == 0 and kb == 0:
                    dE = self.dout("dE%d" % kv, [128, 512])
                    dK = self.dout("dK%d" % kv, [128, 128])
                    P.op("act", lambda e, E=E: e.copy(out=self.yv[1][:, :512], in_=E[:, :512]), reads=[ek], writes=["yv1"])
                    P.op("sp", lambda e, dE=dE: e.dma_start(out=dE, in_=self.yv[1][:, :512]), reads=["yv1"], dma="dbg")
                    P.op("act", lambda e, kv=kv: e.copy(out=self.yv[1][:, :128], in_=KT[:, kv, 0:128]), reads=["KT"], writes=["yv1"])
                    P.op("sp", lambda e, dK=dK: e.dma_start(out=dK, in_=self.yv[1][:, :128]), reads=["yv1"], dma="dbg")

                def pv(e, E=E, kb=kb, kv=kv):
                    for hh in range(4):
                        ins = e.matmul(pb[hh][:, :129], lhsT=E[:, hh * 128:(hh + 1) * 128], rhs=Vaug[:, kb, kv, :],
                                       start=(kb == 0), stop=(kb == gi))
                    return ins
                P.op("pe", pv, reads=[ek, "Vaug", "Vaug1"], writes=["pb0", "pb1", "pb2", "pb3"])
            for hh in range(4):
                h = kv * 4 + hh
                P.op("dve", lambda e, hh=hh: e.reciprocal(out=rinv, in_=pb[hh][:, 128:129]), reads=["pb%d" % hh], writes=["rinv"])
                P.op("dve", lambda e, hh=hh, h=h: e.tensor_scalar(out=ao[:, h * 128:(h + 1) * 128], in0=pb[hh][:, :128],
                                                                  scalar1=rinv[:, 0:1], scalar2=None, op0=ALU.mult),
                     reads=["pb%d" % hh, "rinv"], writes=["ao"])
        for h in range(8):
            P.op("pe", lambda e, h=h: e.transpose(pb[6][:, :128], ao[:, h * 128:(h + 1) * 128], ident[:, :]),
                 reads=["ao", "ident"], writes=["pb6"])
            P.op("act", lambda e, h=h: e.copy(out=gatt[:, h, b0:b0 + 128], in_=pb[6][:, :128]), reads=["pb6"], writes=["gatt"])


def mixer(self, l, ti, t0, tn):
    P = self.P
    is_sample = t0 >= self.NT
    if ti == 0 or is_sample:
        mix_init(self, l, ti, t0, tn, is_sample)
    br = self.cfg.get("branches", ("ret", "hg", "att"))
    branches = []
    if "ret" in br:
        mix_retention(self, l, ti, t0, tn, is_sample)
        P.barrier()
        branches.append(("w_up_ret", "gret", "g_ret"))
    if "hg" in br:
        mix_hgrn(self, l, ti, t0, tn, is_sample)
        P.barrier()
        branches.append(("w_up_hg", "ghg", "g_hg"))
    if "att" in br:
        self.hT_dirty = False
        mix_dsa_proj(self, l, ti, t0, tn, is_sample)
        P.barrier()
        if is_sample:
            mix_dsa_attn_sample(self, l, ti, t0, tn)
        else:
            mix_dsa_attn_prompt(self, l, ti, t0, tn)
        P.barrier()
        if self.hT_dirty:
            self.norm_hT(l, 1, t0, tn)
            P.barrier()
        branches.append(("w_up_att", "gatt", "g_att"))
        if self.cfg.get("debug") and l == 0 and ti == 0:
            dbg = self.dout("dbg", [128, 8 * tn])
            dbq = self.dout("dbq", [128, 8 * tn])
            P.op("act", lambda e, g_=self.m["gatt"]: e.copy(out=self.yv[0][:, :8 * tn].rearrange("p (h t) -> p h t", h=8), in_=g_[:, :, :tn]), reads=["gatt"], writes=["yv0"])
            P.op("sp", lambda e: e.dma_start(out=dbg, in_=self.yv[0][:, :8 * tn]), reads=["yv0"], dma="dbg")
            P.op("act", lambda e, q_=self.m["qTa"]: e.copy(out=self.yv[1][:, :8 * tn].rearrange("p (h t) -> p h t", h=8), in_=q_[:, :, :tn]), reads=["qTa"], writes=["yv1"])
            P.op("sp", lambda e: e.dma_start(out=dbq, in_=self.yv[1][:, :8 * tn]), reads=["yv1"], dma="dbg")
    mix_merge(self, l, ti, t0, tn, branches)


def mix_dsa_attn_sample(self, l, ti, t0, tn):
    P, m, pb = self.P, self.m, self.pb
    self.ar_off = self.ar_attn
    NPG = self.cfg["NPAST"] // 128
    SC = 8
    NCH = 128 // SC
    NS = 128
    nsel = float(self.NSEL_S)
    ones, ident = self.ones, self.ident
    qTa, iqT, wi = m["qTa"], m["iqT"], m["wi"]
    KTs, ikTs, vself, gatt = m["KTs"], m["ikTs"], m["vself"], m["gatt"]
    ptc = carve(self, 2).bitcast(I32)
    ptf = cf32(self, 1)
    idx = carve(self, 2 * NCH).bitcast(I32)
    wib = cf32(self, 16)
    gI = [cf32(self, SC * 128) for _ in range(2)]
    gK = cf32(self, SC * 256)
    gV = cf32(self, SC * 256)
    ikTc = carve(self, SC * NPG)
    KTc = carve(self, SC * 2 * NPG)
    r3 = cf32(self, SC * 16)
    sc = cf32(self, NS + 1)
    tmp = cf32(self, NS + 1)
    maskS = cf32(self, NS + 1)
    Ec = cf32(self, SC * 8)
    small = cf32(self, 16)
    lo, hi, mid, cnt, gg, dd, amax = [small[:, i:i + 1] for i in range(7)]
    srow = cf32(self, max(NPG, 16))
    aos = cf32(self, 128)
    P.op("sp", lambda e: e.dma_start(out=ptc[:NPG, :], in_=self.pt_d.rearrange("o n -> n o"), allow_slow_non_contiguous=True),
         writes=["ptc"], dma="ptc")
    P.op("dve", lambda e: e.tensor_copy(out=ptf[:NPG, :], in_=ptc[:NPG, :]), reads=["ptc"], writes=["ptf"])
    for c in range(NCH):
        P.op("dve", lambda e, c=c: e.tensor_scalar(out=idx[:NPG, c:c + 1], in0=ptf[:NPG, :], scalar1=float(NCH), scalar2=float(c + l * self.cfg["NPOOL"] * NCH),
                                                   op0=ALU.mult, op1=ALU.add), reads=["ptf"], writes=["idx"])
    P.op("pe", lambda e: e.matmul(pb[3][:NPG, :16], lhsT=ones[0:1, :NPG], rhs=wi[0:1, 0, :], start=True, stop=True),
         reads=["ones", "wi"], writes=["pb3"])
    P.op("act", lambda e: e.copy(out=wib[:NPG, :], in_=pb[3][:NPG, :16]), reads=["pb3"], writes=["wib"])
    src_i = self.w["cache_idx_k"].rearrange("l n (c s) d -> (l n c) (s d)", s=SC)
    src_k = self.w["cache_k"].rearrange("l n (c s) k d -> (l n c) (s k d)", s=SC)
    src_v = self.w["cache_v"].rearrange("l n (c s) k d -> (l n c) (s k d)", s=SC)
    iq2 = iqT[:, :, 0]

    def gather(dst, src, c, key):
        P.op("pool", lambda e: e.indirect_dma_start(out=dst[:NPG, :], out_offset=None, in_=src,
                                                    in_offset=bass.IndirectOffsetOnAxis(ap=idx[:NPG, c:c + 1], axis=0)),
             reads=["idx"], writes=[key], dma=key)

    def transposes(srcbuf, skey, nblk, dst, dkey):
        per = max(1, 512 // NPG)
        for i0 in range(0, nblk, per):
            n = min(per, nblk - i0)
            bank, bk = pb[6 + (i0 // per) % 2], "pb%d" % (6 + (i0 // per) % 2)

            def tr(e):
                for i in range(n):
                    ins = e.transpose(bank[:, i * NPG:(i + 1) * NPG], srcbuf[:NPG, (i0 + i) * 128:(i0 + i + 1) * 128], ident[:NPG, :NPG])
                return ins
            P.op("pe", tr, reads=[skey, "ident"], writes=[bk])
            P.op("act", lambda e: e.copy(out=dst[:, i0 * NPG:(i0 + n) * NPG], in_=bank[:, :n * NPG]), reads=[bk], writes=[dkey])
    for c in range(NCH):
        g_, gk_ = gI[c % 2], "gI%d" % (c % 2)
        gather(g_, src_i, c, gk_)
        transposes(g_, gk_, SC, ikTc, "ikTc")
        rb, rk = pb[4 + c % 2], "pb%d" % (4 + c % 2)

        def rel(e):
            for s_ in range(SC):
                ins = e.matmul(rb[:NPG, s_ * 16:(s_ + 1) * 16], lhsT=ikTc[:, s_ * NPG:(s_ + 1) * NPG], rhs=iq2, start=True, stop=True)
            return ins
        P.op("pe", rel, reads=["ikTc", "iqT"], writes=[rk])
        r3v = r3[:NPG, :].rearrange("p (s h) -> p s h", h=16)
        P.op("act", lambda e: e.activation(out=r3[:NPG, :], in_=rb[:NPG, :SC * 16], func=AF.Relu), reads=[rk], writes=["r3"])
        P.op("dve", lambda e: e.tensor_tensor(out=r3v, in0=r3v, in1=wib[:NPG, :].unsqueeze(1).to_broadcast([NPG, SC, 16]), op=ALU.mult),
             reads=["r3", "wib"], writes=["r3"])
        P.op("dve", lambda e: e.tensor_reduce(out=sc[:NPG, c * SC:(c + 1) * SC], in_=r3v, axis=AX.X, op=ALU.add),
             reads=["r3"], writes=["sc"])
    P.op("pe", lambda e: e.matmul(pb[3][:1, :16], lhsT=ikTs[:, 0:1], rhs=iq2, start=True, stop=True), reads=["ikT", "iqT"], writes=["pb3"])
    P.op("act", lambda e: e.activation(out=srow[:1, :16], in_=pb[3][:1, :16], func=AF.Relu), reads=["pb3"], writes=["srow"])
    P.op("dve", lambda e: e.tensor_tensor(out=srow[:1, :16], in0=srow[:1, :16], in1=wi[0:1, 0, :], op=ALU.mult), reads=["srow", "wi"], writes=["srow"])
    P.op("dve", lambda e: e.tensor_reduce(out=tmp[:1, 0:1], in_=srow[:1, :16], axis=AX.X, op=ALU.add), reads=["srow"], writes=["tmp"])
    P.op("pe", lambda e: e.matmul(pb[3][:NPG, 0:1], lhsT=ones[0:1, :NPG], rhs=tmp[0:1, 0:1], start=True, stop=True),
         reads=["ones", "tmp"], writes=["pb3"])
    P.op("act", lambda e: e.copy(out=sc[:NPG, NS:NS + 1], in_=pb[3][:NPG, 0:1]), reads=["pb3"], writes=["sc"])
    P.op("dve", lambda e: e.tensor_reduce(out=amax[:NPG, :], in_=sc[:NPG, :NS + 1], axis=AX.X, op=ALU.max, apply_absolute_value=True),
         reads=["sc"], writes=["small"])
    P.op("pe", lambda e: e.transpose(pb[3][:1, :NPG], amax[:NPG, :], ident[:NPG, :NPG]), reads=["small", "ident"], writes=["pb3"])
    P.op("act", lambda e: e.copy(out=srow[:1, :NPG], in_=pb[3][:1, :NPG]), reads=["pb3"], writes=["srow"])
    P.op("dve", lambda e: e.tensor_reduce(out=tmp[:1, 0:1], in_=srow[:1, :NPG], axis=AX.X, op=ALU.max), reads=["srow"], writes=["tmp"])
    P.op("pe", lambda e: e.matmul(pb[3][:NPG, 0:1], lhsT=ones[0:1, :NPG], rhs=tmp[0:1, 0:1], start=True, stop=True),
         reads=["ones", "tmp"], writes=["pb3"])
    P.op("dve", lambda e: e.tensor_scalar(out=hi[:NPG, :], in0=pb[3][:NPG, 0:1], scalar1=1.0, scalar2=None, op0=ALU.add),
         reads=["pb3"], writes=["small"])
    P.op("dve", lambda e: e.tensor_scalar(out=lo[:NPG, :], in0=hi[:NPG, :], scalar1=-1.0, scalar2=None, op0=ALU.mult),
         reads=["small"], writes=["small"])
    P.op("pool", lambda e: e.affine_select(out=sc[:NPG, NS:NS + 1], in_=sc[:NPG, NS:NS + 1], pattern=[[0, 1]],
                                           compare_op=ALU.is_equal, fill=-1e30, base=0, channel_multiplier=1),
         reads=["sc"], writes=["sc"])
    for it in range(40):
        P.op("dve", lambda e: e.tensor_scalar(out=mid[:NPG, :], in0=lo[:NPG, :], scalar1=hi[:NPG, :], scalar2=0.5, op0=ALU.add, op1=ALU.mult),
             reads=["small"], writes=["small"])
        P.op("dve", lambda e: e.tensor_scalar(out=tmp[:NPG, :NS + 1], in0=sc[:NPG, :NS + 1], scalar1=mid[:NPG, :], scalar2=None, op0=ALU.is_ge),
             reads=["sc", "small"], writes=["tmp"])
        P.op("dve", lambda e: e.tensor_reduce(out=cnt[:NPG, :], in_=tmp[:NPG, :NS + 1], axis=AX.X, op=ALU.add), reads=["tmp"], writes=["small"])
        P.op("pe", lambda e: e.matmul(pb[2][:NPG, 0:1], lhsT=ones[:NPG, :NPG], rhs=cnt[:NPG, :], start=True, stop=True),
             reads=["ones", "small"], writes=["pb2"])
        P.op("dve", lambda e: e.tensor_scalar(out=gg[:NPG, :], in0=pb[2][:NPG, 0:1], scalar1=nsel, scalar2=None, op0=ALU.is_ge),
             reads=["pb2"], writes=["small"])
        P.op("dve", lambda e: e.tensor_tensor(out=dd[:NPG, :], in0=mid[:NPG, :], in1=lo[:NPG, :], op=ALU.subtract), reads=["small"], writes=["small"])
        P.op("dve", lambda e: e.scalar_tensor_tensor(out=lo[:NPG, :], in0=dd[:NPG, :], scalar=gg[:NPG, :], in1=lo[:NPG, :],
                                                     op0=ALU.mult, op1=ALU.add), reads=["small"], writes=["small"])
        P.op("dve", lambda e: e.tensor_tensor(out=dd[:NPG, :], in0=hi[:NPG, :], in1=mid[:NPG, :], op=ALU.subtract), reads=["small"], writes=["small"])
        P.op("dve", lambda e: e.scalar_tensor_tensor(out=hi[:NPG, :], in0=dd[:NPG, :], scalar=gg[:NPG, :], in1=mid[:NPG, :],
                                                     op0=ALU.mult, op1=ALU.add), reads=["small"], writes=["small"])
    P.op("dve", lambda e: e.tensor_scalar(out=maskS[:NPG, :NS + 1], in0=sc[:NPG, :NS + 1], scalar1=lo[:NPG, :], scalar2=None, op0=ALU.is_ge),
         reads=["sc", "small"], writes=["maskS"])
    q2 = [qTa[:, kv * 4:(kv + 1) * 4, 0] for kv in range(2)]
    for c in range(NCH):
        gather(gK, src_k, c, "gK")
        gather(gV, src_v, c, "gV")
        transposes(gK, "gK", SC * 2, KTc, "KTc")
        lbk, lk = pb[4 + c % 2], "pb%d" % (4 + c % 2)

        def lgs(e):
            for i in range(SC * 2):
                ins = e.matmul(lbk[:NPG, i * 4:(i + 1) * 4], lhsT=KTc[:, i * NPG:(i + 1) * NPG], rhs=q2[i % 2], start=True, stop=True)
            return ins
        P.op("pe", lgs, reads=["KTc", "qTa"], writes=[lk])
        Ev = Ec[:NPG, :].rearrange("p (s h) -> p s h", h=8)
        P.op("act", lambda e: e.activation(out=Ec[:NPG, :], in_=lbk[:NPG, :SC * 8], func=AF.Exp), reads=[lk], writes=["Ec"])
        P.op("dve", lambda e: e.tensor_tensor(out=Ev, in0=Ev, in1=maskS[:NPG, c * SC:(c + 1) * SC].unsqueeze(2).to_broadcast([NPG, SC, 8]), op=ALU.mult),
             reads=["Ec", "maskS"], writes=["Ec"])

        def pv(e):
            for s_ in range(SC):
                for kv in range(2):
                    first = (c == 0 and s_ == 0)
                    e.matmul(pb[kv][:4, 0:128], lhsT=Ev[:, s_, kv * 4:(kv + 1) * 4], rhs=gV[:NPG, (s_ * 2 + kv) * 128:(s_ * 2 + kv + 1) * 128],
                             start=first, stop=False)
                    ins = e.matmul(pb[kv][:4, 128:129], lhsT=Ev[:, s_, kv * 4:(kv + 1) * 4], rhs=ones[:NPG, 0:1], start=first, stop=False)
            return ins
        P.op("pe", pv, reads=["Ec", "gV", "ones"], writes=["pb0", "pb1"])
    def lself(e):
        for kv in range(2):
            ins = e.matmul(pb[3][:1, kv * 4:(kv + 1) * 4], lhsT=KTs[:, kv, 0:1], rhs=q2[kv], start=True, stop=True)
        return ins
    P.op("pe", lself, reads=["KT", "qTa"], writes=["pb3"])
    P.op("act", lambda e: e.activation(out=srow[:1, :8], in_=pb[3][:1, :8], func=AF.Exp), reads=["pb3"], writes=["srow"])
    P.op("dve", lambda e: e.tensor_scalar(out=srow[:1, :8], in0=srow[:1, :8], scalar1=maskS[0:1, NS:NS + 1], scalar2=None, op0=ALU.mult),
         reads=["srow", "maskS"], writes=["srow"])

    def pvs(e):
        for kv in range(2):
            e.matmul(pb[kv][:4, 0:128], lhsT=srow[0:1, kv * 4:(kv + 1) * 4], rhs=vself[0:1, kv * 128:(kv + 1) * 128], start=False, stop=True)
            ins = e.matmul(pb[kv][:4, 128:129], lhsT=srow[0:1, kv * 4:(kv + 1) * 4], rhs=ones[0:1, 0:1], start=False, stop=True)
        return ins
    P.op("pe", pvs, reads=["srow", "vself", "ones"], writes=["pb0", "pb1"])
    for kv in range(2):
        P.op("dve", lambda e: e.reciprocal(out=small[:4, 8:9], in_=pb[kv][:4, 128:129]), reads=["pb%d" % kv], writes=["small"])
        P.op("dve", lambda e: e.tensor_scalar(out=aos[:4, :128], in0=pb[kv][:4, 0:128], scalar1=small[:4, 8:9], scalar2=None, op0=ALU.mult),
             reads=["pb%d" % kv, "small"], writes=["aos"])
        P.op("pe", lambda e: e.transpose(pb[6][:, :4], aos[:4, :128], ident[:4, :4]), reads=["aos", "ident"], writes=["pb6"])
        P.op("act", lambda e: e.copy(out=gatt[:, kv * 4:(kv + 1) * 4, 0], in_=pb[6][:, :4]), reads=["pb6"], writes=["gatt"])


Builder.mixer = mixer
```
